# Optimizing a Trainium2 kernel written in Bass

```python
import math
import jax, jax.numpy as jnp
from jax import lax
import numpy as np

D_MODEL = 2048
BATCH = 4
SEQ = 2048
DEPTH = 2
DEC_BATCH = 32
DEC_SEQ = 16
PAST_LEN = 1024

CHUNK = 64
EPS = 1e-6
NEG_INF = -1e30
D_SSM = 2048
SSM_HEAD_DIM = 64
SSM_HEADS = D_SSM // SSM_HEAD_DIM
SSM_GROUPS = 4
SSM_HPG = SSM_HEADS // SSM_GROUPS
SSM_STATE = 128
CONV_W = 4
CONV_DIM = D_SSM + 2 * SSM_GROUPS * SSM_STATE
RET_HEADS = 8
RET_DK = 128
RET_DV = 256
ROPE_BASE = 10000.0
ATT_HEADS = 16
ATT_HEAD_DIM = 128
BAND_CHUNKS = 8
WINDOW = BAND_CHUNKS * CHUNK
REL_CLIP = 256
FFN_HIDDEN = -(-8 * D_MODEL // (3 * 256)) * 256
IN_SIZES = (D_SSM, CONV_DIM, SSM_HEADS,
            RET_HEADS * RET_DK, RET_HEADS * RET_DK, RET_HEADS * RET_DV, RET_HEADS * RET_DV,
            ATT_HEADS * ATT_HEAD_DIM, ATT_HEADS * ATT_HEAD_DIM, ATT_HEADS * ATT_HEAD_DIM,
            D_MODEL, D_MODEL, D_MODEL)
N_IN = sum(IN_SIZES)

kernel_name = 'hybrid_streaming_encoder_step'

F32 = jnp.float32


def _normal(k, shape, scale):
    return scale * jax.random.normal(k, shape, F32)


def _rmsnorm(x, g):
    xf = x.astype(F32)
    y = xf * lax.rsqrt(jnp.mean(xf * xf, axis=-1, keepdims=True) + EPS)
    return (y * g.astype(F32)).astype(x.dtype)


def _head_rms(x, g=None):
    xf = x.astype(F32)
    y = xf * lax.rsqrt(jnp.mean(xf * xf, axis=-1, keepdims=True) + EPS)
    if g is not None:
        y = y * g.astype(F32)
    return y.astype(x.dtype)


def _rotary(x, pos):
    d = x.shape[-1]
    half = d // 2
    inv = ROPE_BASE ** (-jnp.arange(half, dtype=F32) * 2.0 / d)
    ang = pos.astype(F32)[:, None] * inv[None, :]
    cos = jnp.cos(ang)[None, :, None, :]
    sin = jnp.sin(ang)[None, :, None, :]
    xf = x.astype(F32)
    x1, x2 = xf[..., :half], xf[..., half:]
    return jnp.concatenate([x1 * cos - x2 * sin, x2 * cos + x1 * sin], axis=-1).astype(x.dtype)


def _causal_conv(xbc, hist, w, b):
    xpad = jnp.concatenate([hist.astype(xbc.dtype), xbc], axis=1)
    y = lax.conv_general_dilated(xpad, w[:, None, :].astype(xpad.dtype), window_strides=(1,),
                                 padding='VALID', dimension_numbers=('NWC', 'WIO', 'NWC'),
                                 feature_group_count=CONV_DIM)
    return jax.nn.silu(y + b.astype(y.dtype)), xpad[:, -(CONV_W - 1):]


def _ssd(x, dt, a, bm, cm, d_skip, h0):
    b, T = x.shape[:2]
    L = min(T, CHUNK)
    nc = T // L
    xc = x.astype(F32).reshape(b, nc, L, SSM_GROUPS, SSM_HPG, SSM_HEAD_DIM)
    dtc = dt.astype(F32).reshape(b, nc, L, SSM_GROUPS, SSM_HPG)
    bc = bm.astype(F32).reshape(b, nc, L, SSM_GROUPS, SSM_STATE)
    cc = cm.astype(F32).reshape(b, nc, L, SSM_GROUPS, SSM_STATE)
    ag = a.astype(F32).reshape(SSM_GROUPS, SSM_HPG)
    acum = jnp.moveaxis(jnp.cumsum(dtc * ag, axis=2), 2, -1)
    dtt = jnp.moveaxis(dtc, 2, -1)
    tri = jnp.tril(jnp.ones((L, L), bool))
    seg = acum[..., :, None] - acum[..., None, :]
    decay = jnp.where(tri, jnp.exp(jnp.where(tri, seg, 0.0)), 0.0)
    cb = jnp.einsum('bcign,bcjgn->bcgij', cc, bc)
    y = jnp.einsum('bcgkij,bcjgkp->bcigkp', cb[:, :, :, None] * decay * dtt[..., None, :], xc)
    to_end = jnp.exp(acum[..., -1:] - acum) * dtt
    states = jnp.einsum('bcjgn,bcgkj,bcjgkp->bcgkpn', bc, to_end, xc)
    chunk_decay = jnp.exp(acum[..., -1])

    def step(h, inp):
        dec, st = inp
        return dec[..., None, None] * h + st, h

    h_init = h0.astype(F32).reshape(b, SSM_GROUPS, SSM_HPG, SSM_HEAD_DIM, SSM_STATE)
    h_last, h_prev = lax.scan(step, h_init, (jnp.moveaxis(chunk_decay, 1, 0), jnp.moveaxis(states, 1, 0)))
    h_prev = jnp.moveaxis(h_prev, 0, 1)
    y = y + jnp.einsum('bcign,bcgkpn,bcgki->bcigkp', cc, h_prev, jnp.exp(acum))
    y = y + d_skip.astype(F32).reshape(SSM_GROUPS, SSM_HPG)[:, :, None] * xc
    return y.reshape(b, T, D_SSM), h_last.reshape(b, SSM_HEADS, SSM_HEAD_DIM, SSM_STATE)


def _gated_group_rms(y, z, w):
    b, T, _ = y.shape
    t = y.astype(F32) * jax.nn.silu(z.astype(F32))
    t = t.reshape(b, T, SSM_GROUPS, D_SSM // SSM_GROUPS)
    t = t * lax.rsqrt(jnp.mean(t * t, axis=-1, keepdims=True) + EPS)
    return (t.reshape(b, T, D_SSM) * w.astype(F32)).astype(z.dtype)


def _retention(q, k, v, s0):
    b, T, h, dk = q.shape
    dv = v.shape[-1]
    L = min(T, CHUNK)
    nc = T // L
    qc = q.astype(F32).reshape(b, nc, L, h, dk)
    kc = k.astype(F32).reshape(b, nc, L, h, dk)
    vc = v.astype(F32).reshape(b, nc, L, h, dv)
    lam = jnp.log1p(-jnp.exp2(-5.0 - jnp.arange(h, dtype=F32)))
    i = jnp.arange(L, dtype=F32)
    diff = i[:, None] - i[None, :]
    dmat = jnp.where(diff >= 0, jnp.exp(lam[:, None, None] * jnp.maximum(diff, 0.0)), 0.0)
    s = jnp.einsum('bcihd,bcjhd->bchij', qc, kc) * dmat
    o = jnp.einsum('bchij,bcjhv->bcihv', s, vc)
    to_end = jnp.exp(lam[:, None] * (L - 1 - i)[None, :])
    kv = jnp.einsum('bcjhd,hj,bcjhv->bchdv', kc, to_end, vc)
    chunk_decay = jnp.exp(lam * L)

    def step(state, kvc):
        return chunk_decay[None, :, None, None] * state + kvc, state

    s_last, s_prev = lax.scan(step, s0.astype(F32), jnp.moveaxis(kv, 1, 0))
    s_prev = jnp.moveaxis(s_prev, 0, 1)
    o = o + jnp.einsum('bcihd,hi,bchdv->bcihv', qc, jnp.exp(lam[:, None] * (i + 1.0)[None, :]), s_prev)
    return o.reshape(b, T, h, dv), s_last


def _band_attention(q, kb, vb, valid, off, rel_bias):
    b, nc, cq = q.shape[:3]
    kbn = kb.shape[2]
    rel = off + jnp.arange(cq)[:, None] - jnp.arange(kbn)[None, :]
    bias = rel_bias.astype(F32)[:, jnp.clip(rel, -REL_CLIP, REL_CLIP) + REL_CLIP]
    s = jnp.einsum('bcqhd,bckhd->bchqk', q, kb).astype(F32) * (ATT_HEAD_DIM ** -0.5) + bias
    s = jnp.where(valid[None, :, None, None, :], s, NEG_INF)
    pr = jax.nn.softmax(s, axis=-1).astype(vb.dtype)
    o = jnp.einsum('bchqk,bckhd->bcqhd', pr, vb)
    return o.reshape(b, nc * cq, ATT_HEADS * ATT_HEAD_DIM)


def _layer(x, pos, p, conv_hist, ssm_h0, ret_s0, kv_hist):
    b, T, _ = x.shape
    hn = _rmsnorm(x, p['norm_mix'])
    u = hn @ p['w_in']
    offsets = np.cumsum(IN_SIZES)[:-1].tolist()
    (z, xbc, dt_raw, rq, rk, rv, rg, aq, ak, av, ga, gb, gc) = jnp.split(u, offsets, axis=-1)

    xbc, conv_state = _causal_conv(xbc, conv_hist, p['conv_w'], p['conv_b'])
    xs, bm, cm = jnp.split(xbc, [D_SSM, D_SSM + SSM_GROUPS * SSM_STATE], axis=-1)
    dt = jax.nn.softplus(dt_raw.astype(F32) + p['dt_bias'].astype(F32))
    a = -jnp.exp(p['a_log'].astype(F32))
    y_ssm, ssm_state = _ssd(xs.reshape(b, T, SSM_HEADS, SSM_HEAD_DIM), dt, a,
                           bm.reshape(b, T, SSM_GROUPS, SSM_STATE), cm.reshape(b, T, SSM_GROUPS, SSM_STATE),
                           p['d_skip'], ssm_h0)
    ys = _gated_group_rms(y_ssm, z, p['ssm_norm'])

    q_r = _rotary(rq.reshape(b, T, RET_HEADS, RET_DK), pos)
    k_r = _rotary(rk.reshape(b, T, RET_HEADS, RET_DK), pos) * (RET_DK ** -0.5)
    o_r, ret_state = _retention(q_r, k_r, rv.reshape(b, T, RET_HEADS, RET_DV), ret_s0)
    yr = (_head_rms(o_r).reshape(b, T, RET_HEADS * RET_DV) * jax.nn.silu(rg.astype(F32))).astype(x.dtype)

    q_a = _head_rms(aq.reshape(b, T, ATT_HEADS, ATT_HEAD_DIM), p['q_norm'])
    k_a = _head_rms(ak.reshape(b, T, ATT_HEADS, ATT_HEAD_DIM), p['k_norm'])
    v_a = av.reshape(b, T, ATT_HEADS, ATT_HEAD_DIM)
    if kv_hist is None:
        nc = T // CHUNK
        idx = jnp.arange(nc)[:, None] + jnp.arange(BAND_CHUNKS + 1)[None, :]

        def gather_band(t):
            tc = t.reshape(b, nc, CHUNK, ATT_HEADS, ATT_HEAD_DIM)
            tp = jnp.concatenate([jnp.zeros((b, BAND_CHUNKS) + tc.shape[2:], tc.dtype), tc], axis=1)
            return tp[:, idx].reshape(b, nc, (BAND_CHUNKS + 1) * CHUNK, ATT_HEADS, ATT_HEAD_DIM)

        valid = jnp.repeat(idx >= BAND_CHUNKS, CHUNK, axis=1)
        ya = _band_attention(q_a.reshape(b, nc, CHUNK, ATT_HEADS, ATT_HEAD_DIM), gather_band(k_a),
                             gather_band(v_a), valid, BAND_CHUNKS * CHUNK, p['rel_bias'])
        keep = min(WINDOW, T)
        k_state = k_a[:, T - keep:]
        v_state = v_a[:, T - keep:]
    else:
        k_hist, v_hist = kv_hist
        kb = jnp.concatenate([k_hist.astype(k_a.dtype), k_a], axis=1)[:, None]
        vb = jnp.concatenate([v_hist.astype(v_a.dtype), v_a], axis=1)[:, None]
        valid = jnp.ones((1, kb.shape[2]), bool)
        ya = _band_attention(q_a[:, None], kb, vb, valid, k_hist.shape[1], p['rel_bias'])
        k_state = k_a
        v_state = v_a

    m = (jax.nn.sigmoid(ga.astype(F32)) * (ys @ p['w_br_ssm'])
         + jax.nn.sigmoid(gb.astype(F32)) * (yr @ p['w_br_ret'])
         + jax.nn.sigmoid(gc.astype(F32)) * (ya @ p['w_br_att']))
    x = x + (m.astype(x.dtype) @ p['w_out'])

    h2 = _rmsnorm(x, p['norm_ffn'])
    fa, fc = jnp.split(h2 @ p['w_ffn_in'], 2, axis=-1)
    x = x + (jax.nn.silu(fa) * fc) @ p['w_ffn_out']
    return x, (k_state, v_state, ret_state, ssm_state, conv_state)


def setup_inputs(seed: int = 0) -> dict:
    key = jax.random.key(seed)
    ks = jax.random.split(key, 26)
    lc = min(WINDOW, PAST_LEN)
    dt0 = jnp.exp(jax.random.uniform(ks[9], (DEPTH, SSM_HEADS), F32)
                  * (math.log(0.1) - math.log(0.001)) + math.log(0.001))
    return {
        'x_prompt': _normal(ks[0], (BATCH, SEQ, D_MODEL), 1.0),
        'x_sample': _normal(ks[1], (DEC_BATCH, DEC_SEQ, D_MODEL), 1.0),
        'cache_attn_k': _normal(ks[2], (DEPTH, DEC_BATCH, lc, ATT_HEADS, ATT_HEAD_DIM), 1.0),
        'cache_attn_v': _normal(ks[3], (DEPTH, DEC_BATCH, lc, ATT_HEADS, ATT_HEAD_DIM), 1.0),
        'state_ret': _normal(ks[4], (DEPTH, DEC_BATCH, RET_HEADS, RET_DK, RET_DV), 1.0),
        'state_ssm': _normal(ks[5], (DEPTH, DEC_BATCH, SSM_HEADS, SSM_HEAD_DIM, SSM_STATE), 0.1),
        'state_conv': _normal(ks[6], (DEPTH, DEC_BATCH, CONV_W - 1, CONV_DIM), 1.0),
        'norm_mix': 1.0 + _normal(ks[7], (DEPTH, D_MODEL), 0.01),
        'w_in': _normal(ks[8], (DEPTH, D_MODEL, N_IN), D_MODEL ** -0.5),
        'conv_w': _normal(ks[10], (DEPTH, CONV_W, CONV_DIM), 0.5),
        'conv_b': _normal(ks[11], (DEPTH, CONV_DIM), 0.02),
        'dt_bias': dt0 + jnp.log(-jnp.expm1(-dt0)),
        'a_log': jnp.log(jax.random.uniform(ks[12], (DEPTH, SSM_HEADS), F32, 1.0, 16.0)),
        'd_skip': 1.0 + _normal(ks[13], (DEPTH, SSM_HEADS), 0.01),
        'ssm_norm': 1.0 + _normal(ks[14], (DEPTH, D_SSM), 0.01),
        'q_norm': 1.0 + _normal(ks[15], (DEPTH, ATT_HEAD_DIM), 0.01),
        'k_norm': 1.0 + _normal(ks[16], (DEPTH, ATT_HEAD_DIM), 0.01),
        'rel_bias': _normal(ks[17], (DEPTH, ATT_HEADS, 2 * REL_CLIP + 1), 0.1),
        'w_br_ssm': _normal(ks[18], (DEPTH, D_SSM, D_MODEL), D_SSM ** -0.5),
        'w_br_ret': _normal(ks[19], (DEPTH, RET_HEADS * RET_DV, D_MODEL), (RET_HEADS * RET_DV) ** -0.5),
        'w_br_att': _normal(ks[20], (DEPTH, ATT_HEADS * ATT_HEAD_DIM, D_MODEL), (ATT_HEADS * ATT_HEAD_DIM) ** -0.5),
        'w_out': _normal(ks[21], (DEPTH, D_MODEL, D_MODEL), D_MODEL ** -0.5),
        'norm_ffn': 1.0 + _normal(ks[22], (DEPTH, D_MODEL), 0.01),
        'w_ffn_in': _normal(ks[23], (DEPTH, D_MODEL, 2 * FFN_HIDDEN), D_MODEL ** -0.5),
        'w_ffn_out': _normal(ks[24], (DEPTH, FFN_HIDDEN, D_MODEL), FFN_HIDDEN ** -0.5),
    }


def reference(x_prompt, x_sample, cache_attn_k, cache_attn_v, state_ret, state_ssm, state_conv,
              norm_mix, w_in, conv_w, conv_b, dt_bias, a_log, d_skip, ssm_norm, q_norm, k_norm,
              rel_bias, w_br_ssm, w_br_ret, w_br_att, w_out, norm_ffn, w_ffn_in, w_ffn_out):
    bp, tp = x_prompt.shape[:2]
    ts = x_sample.shape[1]
    pos_p = jnp.arange(tp)
    pos_s = PAST_LEN + jnp.arange(ts)
    conv0 = jnp.zeros((bp, CONV_W - 1, CONV_DIM), x_prompt.dtype)
    ssm0 = jnp.zeros((bp, SSM_HEADS, SSM_HEAD_DIM, SSM_STATE), F32)
    ret0 = jnp.zeros((bp, RET_HEADS, RET_DK, RET_DV), F32)
    yp, ys = x_prompt, x_sample
    acc_p = ([], [], [], [], [])
    acc_s = ([], [], [], [], [])
    for l in range(DEPTH):
        p = {'norm_mix': norm_mix[l], 'w_in': w_in[l], 'conv_w': conv_w[l], 'conv_b': conv_b[l],
             'dt_bias': dt_bias[l], 'a_log': a_log[l], 'd_skip': d_skip[l], 'ssm_norm': ssm_norm[l],
             'q_norm': q_norm[l], 'k_norm': k_norm[l], 'rel_bias': rel_bias[l],
             'w_br_ssm': w_br_ssm[l], 'w_br_ret': w_br_ret[l], 'w_br_att': w_br_att[l],
             'w_out': w_out[l], 'norm_ffn': norm_ffn[l], 'w_ffn_in': w_ffn_in[l], 'w_ffn_out': w_ffn_out[l]}
        yp, sp = _layer(yp, pos_p, p, conv0, ssm0, ret0, None)
        ys, ss = _layer(ys, pos_s, p, state_conv[l], state_ssm[l], state_ret[l],
                        (cache_attn_k[l], cache_attn_v[l]))
        for i in range(5):
            acc_p[i].append(sp[i])
            acc_s[i].append(ss[i])
    pk, pv, pr, pssm, pconv = [jnp.stack(a) for a in acc_p]
    sk, sv, sr, sssm, sconv = [jnp.stack(a) for a in acc_s]
    return (yp, ys, pk, pv, pr, pssm, pconv, sk, sv, sr, sssm, sconv)
```

```python
import math
import os
import numpy as np
import concourse.bass as bass
import concourse.mybir as mybir
from concourse.bass_utils import run_bass_kernel_spmd

F32 = mybir.dt.float32
BF16 = mybir.dt.bfloat16
AF = mybir.ActivationFunctionType
ALU = mybir.AluOpType

D = 2048
SEG = 512
NSEG_FULL = 4
EPS = 1e-6
Z0, XBC0, DT0, RQ0, RK0, RV0, RG0, AQ0, AK0, AV0, GA0, GB0, GC0 = (
    0, 2048, 5120, 5152, 6176, 7200, 9248, 11296, 13344, 15392, 17440, 19488, 21536)
NIN = 23584
FFH = 5632
NSLAB = 3
LAM = [math.log1p(-2.0 ** (-5.0 - h)) for h in range(8)]
ATT_SCALE = 128 ** -0.5


class R:
    def __init__(self, name="", excl=False):
        self.name = name
        self.w = None
        self.rs = {}
        self.excl = excl


def _split(reads, writes):
    ex = [r for r in reads if r.excl]
    if not ex:
        return list(reads), list(writes)
    return [r for r in reads if not r.excl], list(writes) + [r for r in ex if r not in writes]


class Sched:
    def __init__(self, nc):
        self.nc = nc
        self.engs = {}
        for name, e in (("pe", nc.tensor), ("act", nc.scalar), ("dve", nc.vector), ("pool", nc.gpsimd), ("sp", nc.sync)):
            self.engs[name] = dict(name=name, e=e, sem=nc.alloc_semaphore(name="s_" + name), cnt=0, seen={})
        self.queues = {}
        for qn, en, nq in (("qsp", "sp", 4), ("qpool", "pool", 4)):
            self.queues[qn] = dict(eng=en, sems=[nc.alloc_semaphore(name=f"{qn}{i}") for i in range(nq)], k=0, cnts=[0] * nq)

    def _wait(self, E, toks):
        best = {}
        for key, sem, v in toks:
            if v <= E["seen"].get(key, 0):
                continue
            if key not in best or best[key][1] < v:
                best[key] = (sem, v)
        for key, (sem, v) in best.items():
            E["e"].wait_ge(sem, v)
            E["seen"][key] = v

    def _deps(self, E, reads, writes, same_ok=False):
        toks = []
        for r in reads:
            if r.w is not None:
                toks.append(r.w)
        for r in writes:
            if r.w is not None:
                toks.append(r.w)
            toks.extend(r.rs.values())
        if same_ok:
            toks = [t for t in toks if t[0] != E["name"]]
        return toks

    def _mark(self, key, tok, reads, writes):
        for r in reads:
            r.rs[key] = tok
        for r in writes:
            r.w = tok
            r.rs = {}

    def op(self, en, fn, reads=(), writes=()):
        self.group(en, [fn], reads, writes)

    def group(self, en, fns, reads=(), writes=()):
        reads, writes = _split(reads, writes)
        E = self.engs[en]
        self._wait(E, self._deps(E, reads, writes, same_ok=(en == "pe")))
        ins = None
        for fn in fns:
            ins = fn()
        E["cnt"] += 1
        ins.then_inc(E["sem"], 1)
        self._mark(E["name"], (E["name"], E["sem"], E["cnt"]), reads, writes)

    def dma(self, qn, out, in_, reads=(), writes=(), **kw):
        Q = self.queues[qn]
        E = self.engs[Q["eng"]]
        nq = len(Q["sems"])
        i = Q["k"] % nq
        toks = self._deps(E, reads, writes)
        key = f"{qn}{i}"
        if Q["cnts"][i] > 0:
            toks.append((key, Q["sems"][i], Q["cnts"][i]))
        self._wait(E, toks)
        ins = E["e"].dma_start(out=out, in_=in_, **kw)
        Q["cnts"][i] += 16
        Q["k"] += 1
        ins.then_inc(Q["sems"][i], 16)
        self._mark(key, (key, Q["sems"][i], Q["cnts"][i]), reads, writes)

    def _all_toks(self):
        toks = []
        for e in self.engs.values():
            if e["cnt"] > 0:
                toks.append((e["name"], e["sem"], e["cnt"]))
        for qn, Q in self.queues.items():
            for i, s in enumerate(Q["sems"]):
                if Q["cnts"][i] > 0:
                    toks.append((f"{qn}{i}", s, Q["cnts"][i]))
        return toks

    def barrier(self, engines=("pe", "act", "dve", "sp", "pool")):
        toks = self._all_toks()
        for en in engines:
            self._wait(self.engs[en], toks)

    def finish(self):
        self._wait(self.engs["sp"], self._all_toks())


def build(n_layers=2, n_segs=4):
    nc = bass.Bass("TRN2", target_bir_lowering=False)
    S = Sched(nc)

    def din(name, shape):
        return nc.dram_tensor(name, list(shape), F32, kind="ExternalInput").ap()

    def dout(name, shape):
        return nc.dram_tensor(name, list(shape), F32, kind="ExternalOutput").ap()

    def dscr(name, shape, dt):
        return nc.dram_tensor(name, list(shape), dt, kind="Internal").ap()

    def sb(name, shape, dt=F32):
        return nc.alloc_sbuf_tensor(name, list(shape), dt).ap()

    xp_d = din("x_prompt", [2048, D])
    xs_d = din("x_sample", [4, 16, D])
    ck_d = din("cache_attn_k", [2, 4, 512, 16, 128])
    cv_d = din("cache_attn_v", [2, 4, 512, 16, 128])
    sret_d = din("state_ret", [2, 4, 8, 128, 256])
    sssm_d = din("state_ssm", [2, 4, 32, 64, 128])
    sconv_d = din("state_conv", [2, 4, 3, 3072])
    norm_mix_d = din("norm_mix", [2, D])
    w_in_d = din("w_in", [2, D, NIN])
    conv_w_d = din("conv_w", [2, 4, 3072])
    conv_b_d = din("conv_b", [2, 3072])
    dt_bias_d = din("dt_bias", [2, 32])
    a_log_d = din("a_log", [2, 32])
    d_skip_d = din("d_skip", [2, 32])
    ssm_norm_d = din("ssm_norm", [2, D])
    q_norm_d = din("q_norm", [2, 128])
    k_norm_d = din("k_norm", [2, 128])
    w_br_ssm_d = din("w_br_ssm", [2, D, D])
    w_br_ret_d = din("w_br_ret", [2, D, D])
    w_br_att_d = din("w_br_att", [2, D, D])
    w_out_d = din("w_out", [2, D, D])
    norm_ffn_d = din("norm_ffn", [2, D])
    w_ffn_in_d = din("w_ffn_in", [2, D, 2 * FFH])
    w_ffn_out_d = din("w_ffn_out", [2, FFH, D])
    rope_d = din("c_rope", [4, 4, 128, 576])
    bias_d = din("c_biasT", [2, 16, 576, 64])
    biasr_d = din("c_biasR", [2, 16, 576, 64])
    dm_d = din("c_dm", [64, 512])
    g1_d = din("c_g1", [128, 512])
    te_d = din("c_te", [64, 16])
    ident_d = din("c_ident", [128, 128])
    tri_d = din("c_tri", [64, 64])
    pswap_d = din("c_pswap", [128, 128])

    y_p = dout("y_p", [2048, D])
    y_s = dout("y_s", [64, D])
    k_p = dout("k_p", [2, 512, 16, 128])
    v_p = dout("v_p", [2, 512, 16, 128])
    ret_p = dout("ret_p", [2, 8, 128, 256])
    ssm_p = dout("ssm_p", [2, 2048, 128])
    conv_p = dout("conv_p", [2, 3, 3072])
    k_s = dout("k_s", [2, 64, 16, 128])
    v_s = dout("v_s", [2, 64, 16, 128])
    ret_s = dout("ret_s", [2, 4, 8, 128, 256])
    ssm_s = dout("ssm_s", [2, 4, 2048, 128])
    conv_s = dout("conv_s", [2, 4, 3, 3072])

    hscr = dscr("hscr", [2, 128, 2048], F32); r_hscr = [R(), R()]
    sscr = dscr("sscr", [2, 128, 2048], F32); r_sscr = [R(), R()]
    khd = dscr("khd", [2, 2, 128, 16, 512], BF16)
    vhd = dscr("vhd", [2, 2, 128, 16, 512], BF16)
    r_khd = [[[R() for _ in range(4)] for _ in range(2)] for _ in range(2)]
    r_vhd = [[[R() for _ in range(4)] for _ in range(2)] for _ in range(2)]

    NCOLMAX = 576
    xT = sb("xT", [128, 16, NCOLMAX]); r_x = [R() for _ in range(16)]
    hnT = sb("hnT", [128, 16, NCOLMAX], BF16); r_hn = [R() for _ in range(16)]
    br = sb("br", [128, 16, NCOLMAX], BF16); r_br = [R() for _ in range(16)]
    mT = sb("mT", [128, 16, NCOLMAX], BF16); r_m = [R() for _ in range(16)]
    slabs = [sb(f"slab{i}", [128, 8192], BF16) for i in range(NSLAB)]
    r_slab = [R() for _ in range(NSLAB)]
    ident = sb("ident", [128, 128]); identb = sb("identb", [128, 128], BF16)
    meanD = sb("meanD", [128, 128], BF16); mean512 = sb("mean512", [128, 128], BF16)
    mean256 = sb("mean256", [128, 128], BF16); mean128 = sb("mean128", [128, 128], BF16)
    pswapf = sb("pswapf", [128, 128]); pswapb = sb("pswapb", [128, 128], BF16)
    ones64b = sb("ones64b", [64, 128], BF16); onesf = sb("onesf", [64, 128])
    tri = sb("tri", [64, 64]); dm = sb("dm", [64, 512]); g1 = sb("g1", [128, 512]); te = sb("te", [64, 16])
    r_c = R("consts")
    gm = sb("gm", [128, 16]); gf = sb("gf", [128, 16]); cw = sb("cw", [128, 24, 4]); cb = sb("cb", [128, 24])
    nw = sb("nw", [128, 16]); dcol = sb("dcol", [128, 16]); qn = sb("qn", [128, 1]); kn = sb("kn", [128, 1])
    dtb = sb("dtb", [64, 32]); abc = sb("abc", [64, 32])
    convc = sb("convc", [128, 2, 24, 3]); r_convc = [R(), R()]
    r_par = R("params")
    UN = None
    U_holder = []

    class Carver:
        def __init__(self):
            self.o = 0

        def F(self, n, parts=128):
            U = U_holder[0]
            a = U[0:parts, self.o:self.o + n]
            self.o += (n + 7) // 8 * 8
            assert self.o <= U_holder[1], (self.o, U_holder[1])
            return a

        def B(self, n, parts=128):
            U = U_holder[0]
            w = (n + 1) // 2
            a = U[0:parts, self.o:self.o + w].bitcast(BF16)[:, 0:n]
            self.o += (w + 7) // 8 * 8
            assert self.o <= U_holder[1], (self.o, U_holder[1])
            return a

    pj = [nc.alloc_psum_tensor(f"pj{i}", [128, 512], F32).ap() for i in range(2)]
    r_pj = [R(excl=True) for _ in range(2)]
    pjs_t = nc.alloc_psum_tensor("pjs", [128, 512], F32).ap()
    _rpjs = R(excl=True)
    r_pjs = [_rpjs for _ in range(8)]
    aux = [nc.alloc_psum_tensor(f"aux{i}", [128, 512], F32).ap() for i in range(4)]
    r_aux = [R(excl=True) for _ in range(4)]
    trb_t = nc.alloc_psum_tensor("trb", [128, 1024], BF16).ap()
    _rtrb = R(excl=True)
    r_trb = [_rtrb, _rtrb]
    rot = dict(pj=0, pjs=0, aux=0, trb=0)

    def next_pj():
        i = rot["pj"] % 2; rot["pj"] += 1
        return pj[i], r_pj[i]

    def next_pjs():
        i = rot["pjs"] % 8; rot["pjs"] += 1
        return pjs_t[:, i * 64:(i + 1) * 64], r_pjs[i]

    def next_aux():
        i = rot["aux"] % 4; rot["aux"] += 1
        return aux[i], r_aux[i]

    def next_trb():
        i = rot["trb"] % 2; rot["trb"] += 1
        return trb_t[:, i * 512:(i + 1) * 512], r_trb[i]

    def ps_for(tn):
        return next_pj() if tn > 64 else next_pjs()

    UN = (nc.sbuf_bytes_remaining - 512) // 4
    U_holder.append(sb("U", [128, UN]))
    U_holder.append(UN)
    V = nc.vector
    A = nc.scalar
    PE = nc.tensor

    def vop(fn, reads, writes):
        S.op("dve", fn, reads, writes)

    def aop(fn, reads, writes):
        S.op("act", fn, reads, writes)

    def act(out, in_, func, reads, writes, **kw):
        S.op("act", lambda: A.activation(out=out, in_=in_, func=func, **kw), reads, writes)

    def vtt(out, in0, in1, op, reads, writes):
        S.op("dve", lambda: V.tensor_tensor(out=out, in0=in0, in1=in1, op=op), reads, writes)

    def vcopy(out, in_, reads, writes):
        S.op("dve", lambda: V.tensor_copy(out=out, in_=in_), reads, writes)

    def acopy(out, in_, reads, writes):
        S.op("act", lambda: A.copy(out=out, in_=in_), reads, writes)

    def mm(out, lhsT, rhs, start=True, stop=True, skip=False):
        if skip:
            return lambda: PE.matmul(out, lhsT=lhsT, rhs=rhs, start=start, stop=stop, skip_group_check=True)
        return lambda: PE.matmul(out, lhsT=lhsT, rhs=rhs, start=start, stop=stop)

    def tr(out, in_, idn):
        return lambda: PE.transpose(out, in_, idn)

    S.dma("qsp", ident[:], ident_d[:, :], writes=[r_c])
    S.dma("qsp", tri[:], tri_d[:, :], writes=[r_c])
    S.dma("qsp", pswapf[:], pswap_d[:, :], writes=[r_c])
    S.dma("qsp", dm[:], dm_d[:, :], writes=[r_c])
    S.dma("qsp", g1[:], g1_d[:, :], writes=[r_c])
    S.dma("qsp", te[:], te_d[:, :], writes=[r_c])
    vcopy(identb[:], ident[:], [r_c], [r_c])
    vcopy(pswapb[:], pswapf[:], [r_c], [r_c])
    vop(lambda: V.memset(meanD[:], 1.0 / 2048), [], [r_c])
    vop(lambda: V.memset(mean512[:], 1.0 / 512), [], [r_c])
    vop(lambda: V.memset(mean256[:], 1.0 / 256), [], [r_c])
    vop(lambda: V.memset(mean128[:], 1.0 / 128), [], [r_c])
    vop(lambda: V.memset(ones64b[:], 1.0), [], [r_c])
    vop(lambda: V.memset(onesf[:], 1.0), [], [r_c])

    def colvec(dst, src_vec, nchunk):
        S.dma("qsp", dst, src_vec.rearrange("(c p) -> p c", p=128), writes=[r_par], allow_slow_non_contiguous=True)

    def load_params(l):
        S.barrier()
        colvec(gm[:], norm_mix_d[l], 16)
        colvec(gf[:], norm_ffn_d[l], 16)
        colvec(nw[:], ssm_norm_d[l], 16)
        colvec(cb[:], conv_b_d[l], 24)
        for k in range(4):
            S.dma("qsp", cw[:, :, k], conv_w_d[l, k].rearrange("(c p) -> p c", p=128), writes=[r_par], allow_slow_non_contiguous=True)
        S.dma("qsp", qn[:], q_norm_d[l].rearrange("(p o) -> p o", o=1), writes=[r_par], allow_slow_non_contiguous=True)
        S.dma("qsp", kn[:], k_norm_d[l].rearrange("(p o) -> p o", o=1), writes=[r_par], allow_slow_non_contiguous=True)
        dsv = d_skip_d[l].rearrange("(c two) -> two c", two=2)
        for hh in range(2):
            S.dma("qsp", dcol[64 * hh:64 * hh + 64, :], dsv[hh:hh + 1, :].partition_broadcast(64), writes=[r_par],
                  allow_slow_non_contiguous=True)
        S.dma("qsp", dtb[:], dt_bias_d[l:l + 1, :].partition_broadcast(64), writes=[r_par])
        S.dma("qsp", abc[:], a_log_d[l:l + 1, :].partition_broadcast(64), writes=[r_par])
        act(abc[:], abc[:], AF.Exp, [r_par], [r_par])
        vop(lambda: V.tensor_scalar(out=abc[:], in0=abc[:], scalar1=-1.0, scalar2=None, op0=ALU.mult), [r_par], [r_par])

    def run_jobs(jobs):
        loads = [i for i, j in enumerate(jobs) if j[0] is not None]
        slot_of = {ji: n % NSLAB for n, ji in enumerate(loads)}
        st = dict(nxt=0)

        def issue_next():
            if st["nxt"] < len(loads):
                ji = loads[st["nxt"]]
                jobs[ji][0](slot_of[ji])
                st["nxt"] += 1

        for _ in range(NSLAB):
            issue_next()
        for ji, (ld, fn) in enumerate(jobs):
            if ld is not None:
                fn(slot_of[ji])
                issue_next()
            else:
                fn(None)

    def wview(w2d):
        return w2d.rearrange("(kc p) c -> p kc c", p=128)

    def proj_job(jobs, wv, c0, ncols, rhs_buf, rhs_regs, tiles, consumer):
        def ld(slot):
            sv = slabs[slot][:, 0:16 * ncols].rearrange("p (k c) -> p k c", k=16)
            S.dma("qpool", sv, wv[:, :, c0:c0 + ncols], writes=[r_slab[slot]])

        def fn(slot):
            sv = slabs[slot][:, 0:16 * ncols].rearrange("p (k c) -> p k c", k=16)
            for cc in range(ncols // 128):
                outs = []
                for (t0, tn) in tiles:
                    ps, rps = ps_for(tn)
                    S.group("pe", [mm(ps[:, 0:tn], sv[:, k, cc * 128:(cc + 1) * 128], rhs_buf[:, k, t0:t0 + tn],
                                      k == 0, k == 15) for k in range(16)],
                            [r_slab[slot]] + rhs_regs, [rps])
                    outs.append((t0, tn, ps, rps))
                consumer(cc, outs)
        jobs.append((ld, fn))

    def layer_pass(l, seg, last_layer):
        has_s = (seg == 0)
        last_seg = (seg == n_segs - 1)
        ncol = 576 if has_s else 512
        tiles = [(0, 512)] + ([(512, 64)] if has_s else [])
        pchunks = [(64 * c, 64) for c in range(8)]
        schunks = [(512 + 16 * i, 16) for i in range(4)] if has_s else []
        allq = pchunks + schunks
        w_in_v = wview(w_in_d[l])
        jobs = []

        def J(fn):
            jobs.append((None, lambda _s, fn=fn: fn()))

        def sumsq_rstd(srcs, meanm, rstd, tmpsq, r_tmpsq, reads_of):
            pss = [(t0, tn) + ps_for(tn) for (t0, tn) in tiles]
            n = len(srcs)
            for i, (src, rr) in enumerate(zip(srcs, reads_of)):
                sq = tmpsq[i % 2]
                act(sq[:, 0:ncol], src, AF.Square, [rr], [r_tmpsq[i % 2]])
                for (t0, tn, ps, rps) in pss:
                    S.group("pe", [mm(ps[:, 0:tn], meanm[:], sq[:, t0:t0 + tn], i == 0, i == n - 1)], [r_tmpsq[i % 2], r_c], [rps])
            for (t0, tn, ps, rps) in pss:
                act(rstd[:, t0:t0 + tn], ps[:, 0:tn], AF.Ln, [rps], [r_rstd], bias=EPS, scale=1.0)
            act(rstd[:, 0:ncol], rstd[:, 0:ncol], AF.Exp, [r_rstd], [r_rstd], scale=-0.5)

        r_rstd = R()

        def phase0():
            load_params(l)
            cv = Carver()
            xin = cv.F(2048)
            r_xin = R()
            if l == 0:
                srcs = [(xp_d[seg * 512 + 128 * tt: seg * 512 + 128 * tt + 128, :], 128, 128 * tt) for tt in range(4)]
                if has_s:
                    srcs.append((xs_d.rearrange("b t d -> (b t) d"), 64, 512))
                for (src, nt, c0) in srcs:
                    S.dma("qsp", xin[0:nt, :], src, writes=[r_xin])
                    for q in range(4):
                        ps, rps = next_aux()
                        S.group("pe", [tr(ps[:, j * nt:(j + 1) * nt], xin[0:nt, (4 * q + j) * 128:(4 * q + j + 1) * 128], ident[0:nt, 0:nt])
                                       for j in range(4)], [r_xin, r_c], [rps])
                        vcopy(xT[:, 4 * q:4 * q + 4, c0:c0 + nt], ps[:, 0:4 * nt].rearrange("p (j t) -> p j t", j=4),
                              [rps], r_x[4 * q:4 * q + 4])
            rstd = cv.F(576)
            tsq = [cv.B(576), cv.B(576)]
            rts = [R(), R()]
            sumsq_rstd([xT[:, c, 0:ncol] for c in range(16)], meanD, rstd, tsq, rts, r_x)
            for c in range(16):
                vop(lambda c=c: V.scalar_tensor_tensor(out=hnT[:, c, 0:ncol], in0=xT[:, c, 0:ncol], scalar=gm[:, c:c + 1],
                                                       in1=rstd[:, 0:ncol], op0=ALU.mult, op1=ALU.mult),
                    [r_x[c], r_par, r_rstd], [r_hn[c]])
            S.barrier()
        J(phase0)
        marks = {"p0": len(jobs)}

        def branch_jobs(cv, w_br, gate0, first):
            cv.o = max(cv.o, int(os.environ.get("KGOFF", 0)))
            G = [cv.F(4 * 576), cv.F(4 * 576)]
            rG = [R(), R()]
            tmp = [cv.F(576), cv.F(576)]
            rtmp = [R(), R()]
            wv = wview(w_br[l])
            for j in range(4):
                Gj = G[j % 2].rearrange("p (c t) -> p c t", c=4)

                def gcons(cc, outs, Gj=Gj, j=j):
                    for (t0, tn, ps, rps) in outs:
                        if os.environ.get("KNOG"):
                            continue
                        act(Gj[:, cc, t0:t0 + tn], ps[:, 0:tn], AF.Sigmoid, [rps], [rG[j % 2]])
                kbr = os.environ.get("KBR", "gb")
                if "g" in kbr:
                    g0x = int(os.environ.get("KGATE0", gate0))
                    proj_job(jobs, w_in_v, g0x + 512 * j, 512, hnT, r_hn, tiles, gcons)

                def bcons(cc, outs, Gj=Gj, j=j):
                    c = 4 * j + cc
                    for (t0, tn, ps, rps) in outs:
                        if first:
                            vtt(mT[:, c, t0:t0 + tn], ps[:, 0:tn], Gj[:, cc, t0:t0 + tn], ALU.mult, [rps, rG[j % 2]], [r_m[c]])
                        else:
                            i = rot["aux"] % 2
                            vtt(tmp[i][:, 0:tn], ps[:, 0:tn], Gj[:, cc, t0:t0 + tn], ALU.mult, [rps, rG[j % 2]], [rtmp[i]])
                            vtt(mT[:, c, t0:t0 + tn], mT[:, c, t0:t0 + tn], tmp[i][:, 0:tn], ALU.add, [rtmp[i], r_m[c]], [r_m[c]])
                            rot["aux"] += 1
                if "b" in kbr:
                    proj_job(jobs, wv, 512 * j, 512, br, r_br, tiles, bcons)

        cv1 = Carver()
        xpb = [cv1.F(591), cv1.F(591)]; r_xp = [R(), R()]
        cacc = [cv1.F(576), cv1.F(576)]; r_cacc = [R(), R()]
        hs = cv1.F(288).rearrange("p (c b r) -> p c b r", c=24, b=4); r_hs = R()
        off_hso = cv1.o
        hso = cv1.F(288).rearrange("p (c b r) -> p c b r", c=24, b=4); r_hso = R()
        dtt = cv1.F(384, 64).rearrange("p (q h) -> p q h", q=12)
        dtA = cv1.F(384, 64).rearrange("p (q h) -> p q h", q=12)
        r_dt = R()
        acum = cv1.F(32, 64); wsd = cv1.F(32, 64); wdt = cv1.F(32, 64); etot = cv1.F(32); r_ch = R()
        Rb = cv1.F(512, 64); r_Rb = R()
        Db = cv1.F(512, 64); r_Db = R()
        Eb = Db; r_Eb = r_Db
        eA = cv1.F(512); r_eA = R()
        CBm = cv1.F(64, 64); r_CBm = R()
        hT = cv1.F(2048); r_hT = [R() for _ in range(4)]
        stg = cv1.F(2048); r_stg = R()
        BT = cv1.B(4 * 576).rearrange("p (g t) -> p g t", g=4); r_BT = [R() for _ in range(4)]
        CT = cv1.B(4 * 576).rearrange("p (g t) -> p g t", g=4); r_CT = [R() for _ in range(4)]
        dtw = cv1.B(512).rearrange("p (k c) -> p k c", k=16); r_dtw = R()
        MT = cv1.B(512, 64); r_MT = R()
        xt = cv1.B(512, 64); r_xt = R()
        xd = cv1.B(512, 64); r_xd = R()
        xs2 = cv1.B(512, 64); r_xs2 = R()
        Cp = cv1.B(512); r_Cp = R()
        ytb = cv1.B(512, 64); r_ytb = R()
        hTb = cv1.B(2048); r_hTb = [R() for _ in range(4)]
        Btok = cv1.B(128, 64); r_Btok = R()
        cv1b = Carver()
        rstd1 = cv1b.F(576)
        tsq1 = [cv1b.B(576), cv1b.B(576)]; rts1 = [R(), R()]
        cvc = Carver()
        Rb_b = cvc.F(512, 64); Db_b = cvc.F(512, 64); eA_b = cvc.F(512); CBm_b = cvc.F(64, 64)
        MT_b = cvc.B(512, 64); xt_b = cvc.B(512, 64); xd_b = cvc.B(512, 64); xs2_b = cvc.B(512, 64)
        assert cvc.o <= off_hso, (cvc.o, off_hso)
        Cp_b = cv1.B(512); ytb_b = cv1.B(512, 64); Btok_b = cv1.B(128, 64)
        rD_b = R()
        ssd_sets = [
            (Rb, r_Rb, Db, r_Db, Eb, r_Eb, eA, r_eA, CBm, r_CBm, MT, r_MT, xt, r_xt, xd, r_xd, xs2, r_xs2, Cp, r_Cp, ytb, r_ytb,
             Btok, r_Btok),
            (Rb_b, R(), Db_b, rD_b, Db_b, rD_b, eA_b, R(), CBm_b, R(), MT_b, R(), xt_b, R(), xd_b, R(), xs2_b, R(), Cp_b, R(),
             ytb_b, R(), Btok_b, R()),
        ]
        szt = tsq1; r_szt = rts1

        def ssd_pre():
            S.dma("qpool", dtw, w_in_v[:, :, DT0:DT0 + 32], writes=[r_dtw])
            if has_s:
                for b in range(4):
                    for r in range(3):
                        S.dma("qsp", hs[:, :, b, r], sconv_d[l, b, r].rearrange("(c p) -> p c", p=128), writes=[r_hs],
                              allow_slow_non_contiguous=True)
        J(ssd_pre)

        for j in range(6):
            def xcons(cc, outs, j=j):
                ci = 4 * j + cc
                i = ci % 2
                xp = xpb[i]
                xps = xp[:, 515:591].rearrange("p (b t) -> p b t", b=4)
                for (t0, tn, ps, rps) in outs:
                    if t0 == 0:
                        acopy(xp[:, 3:515], ps[:, 0:512], [rps], [r_xp[i]])
                    else:
                        acopy(xps[:, :, 3:19], ps[:, 0:64].rearrange("p (b t) -> p b t", b=4), [rps], [r_xp[i]])
                if seg == 0:
                    vop(lambda: V.memset(xp[:, 0:3], 0.0), [], [r_xp[i]])
                else:
                    vcopy(xp[:, 0:3], convc[:, l, ci, :], [r_convc[l]], [r_xp[i]])
                vcopy(convc[:, l, ci, :], xp[:, 512:515], [r_xp[i]], [r_convc[l]])
                if has_s:
                    vcopy(xps[:, :, 0:3], hs[:, ci, :, :], [r_hs], [r_xp[i]])
                    vcopy(hso[:, ci, :, :], xps[:, :, 16:19], [r_xp[i]], [r_hso])
                acc = cacc[i]
                accs = acc[:, 512:576].rearrange("p (b t) -> p b t", b=4)
                vop(lambda: V.tensor_scalar(out=acc[:, 0:512], in0=xp[:, 0:512], scalar1=cw[:, ci, 0:1], scalar2=None, op0=ALU.mult),
                    [r_xp[i], r_par], [r_cacc[i]])
                for k in range(1, 4):
                    vop(lambda k=k: V.scalar_tensor_tensor(out=acc[:, 0:512], in0=xp[:, k:k + 512], scalar=cw[:, ci, k:k + 1],
                                                           in1=acc[:, 0:512], op0=ALU.mult, op1=ALU.add),
                        [r_xp[i], r_par], [r_cacc[i]])
                if has_s:
                    vop(lambda: V.tensor_scalar(out=accs, in0=xps[:, :, 0:16], scalar1=cw[:, ci, 0:1], scalar2=None, op0=ALU.mult),
                        [r_xp[i], r_par], [r_cacc[i]])
                    for k in range(1, 4):
                        vop(lambda k=k: V.scalar_tensor_tensor(out=accs, in0=xps[:, :, k:k + 16], scalar=cw[:, ci, k:k + 1],
                                                               in1=accs, op0=ALU.mult, op1=ALU.add),
                            [r_xp[i], r_par], [r_cacc[i]])
                if ci < 16:
                    dst, rd = br[:, ci, 0:ncol], r_br[ci]
                elif ci < 20:
                    dst, rd = BT[:, ci - 16, 0:ncol], r_BT[ci - 16]
                else:
                    dst, rd = CT[:, ci - 20, 0:ncol], r_CT[ci - 20]
                act(dst, acc[:, 0:ncol], AF.Silu, [r_cacc[i], r_par], [rd], bias=cb[:, ci:ci + 1], scale=1.0)
            proj_job(jobs, w_in_v, XBC0 + 512 * j, 512, hnT, r_hn, tiles, xcons)

        def ssd_dt():
            nq = len(allq)
            ps, rps = next_aux()
            for q, (c0, L) in enumerate(allq):
                S.group("pe", [mm(ps[0:L, 32 * q:32 * q + 32], hnT[:, k, c0:c0 + L], dtw[:, k, :], k == 0, k == 15) for k in range(16)],
                        r_hn + [r_dtw], [rps])
            psv = ps[0:64, 0:32 * nq].rearrange("p (q h) -> p q h", q=nq)
            vtt(dtt[:, 0:nq, :], psv, dtb[:].unsqueeze(1).to_broadcast([64, nq, 32]), ALU.add, [rps, r_par], [r_dt])
            act(dtt[:, 0:nq, :], dtt[:, 0:nq, :], AF.Exp, [r_dt], [r_dt])
            act(dtt[:, 0:nq, :], dtt[:, 0:nq, :], AF.Ln, [r_dt], [r_dt], bias=1.0, scale=1.0)
            vtt(dtA[:, 0:nq, :], dtt[:, 0:nq, :], abc[:].unsqueeze(1).to_broadcast([64, nq, 32]), ALU.mult, [r_dt, r_par], [r_dt])
        marks["conv"] = len(jobs)
        J(ssd_dt)
        J(S.barrier)
        marks["dt"] = len(jobs)

        chs = [dict(acum=acum, wsd=wsd, wdt=wdt, etot=etot, r=r_ch)]
        chs.append(dict(acum=cv1.F(32, 64), wsd=cv1.F(32, 64), wdt=cv1.F(32, 64), etot=cv1.F(32), r=R()))
        slot_aux = [dict(i=0), dict(i=0)]

        def chain_aux(slot):
            k = 2 * slot + (slot_aux[slot]["i"] % 2)
            slot_aux[slot]["i"] += 1
            return aux[k], r_aux[k]

        def ssd_prologue(q, c0, L, slot):
            ch = chs[q % 2]
            acum, wsd, wdt, etot, r_ch = ch["acum"], ch["wsd"], ch["wdt"], ch["etot"], ch["r"]
            ps1, rps1 = chain_aux(slot)
            S.group("pe", [mm(ps1[0:L, 0:32], tri[0:L, 0:L], dtA[0:L, q, :]),
                           mm(ps1[:, 32:64], onesf[0:L, :], dtA[0:L, q, :])], [r_c, r_dt], [rps1])
            vcopy(acum[0:L, :], ps1[0:L, 0:32], [rps1], [r_ch])
            act(etot[:, :], ps1[:, 32:64], AF.Exp, [rps1], [r_ch])
            vtt(wsd[0:L, :], ps1[0:L, 32:64], acum[0:L, :], ALU.subtract, [rps1, r_ch], [r_ch])
            act(wsd[0:L, :], wsd[0:L, :], AF.Exp, [r_ch], [r_ch])
            vtt(wdt[0:L, :], wsd[0:L, :], dtt[0:L, q, :], ALU.mult, [r_ch, r_dt], [r_ch])

        def ssd_gchain(q, c0, L, has_state, g, slot):
            cols = slice(c0, c0 + L)
            if g == 0:
                ssd_prologue(q, c0, L, slot)
            ch = chs[q % 2]
            acum, wdt, etot, r_ch = ch["acum"], ch["wdt"], ch["etot"], ch["r"]
            (Rb, r_Rb, Db, r_Db, Eb, r_Eb, eA, r_eA, CBm, r_CBm, MT, r_MT, xt, r_xt, xd, r_xd, xs2, r_xs2, Cp, r_Cp, ytb, r_ytb,
             Btok, r_Btok) = ssd_sets[slot]

            def v3(buf, parts=None):
                p = L if parts is None else parts
                return buf[0:p, 0:8 * L].rearrange("p (h i) -> p h i", h=8)
            x3 = lambda b: b[0:L, :].rearrange("p (h d) -> p h d", h=8)
            vtt(v3(Rb), tri[0:L, 0:L].unsqueeze(1).to_broadcast([L, 8, L]),
                dtA[0:L, q, 8 * g:8 * g + 8].unsqueeze(2).to_broadcast([L, 8, L]), ALU.mult, [r_c, r_dt], [r_Rb])
            yield
            psA, rpsA = chain_aux(slot)
            S.group("pe", [mm(psA[:, 0:8 * L], onesf[0:L, :], Rb[0:L, 0:8 * L])], [r_c, r_Rb], [rpsA])
            pscb, rpscb = chain_aux(slot)
            S.group("pe", [mm(pscb[0:L, 0:L], BT[:, g, cols], CT[:, g, cols])], [r_BT[g], r_CT[g]], [rpscb])
            yield
            vtt(CBm[0:L, 0:L], pscb[0:L, 0:L], tri[0:L, 0:L], ALU.mult, [rpscb, r_c], [r_CBm])
            vtt(v3(Db), v3(psA), acum[0:L, 8 * g:8 * g + 8].unsqueeze(2).to_broadcast([L, 8, L]), ALU.subtract,
                [rpsA, r_ch], [r_Db])
            yield
            vop(lambda: V.tensor_scalar(out=Db[0:L, 0:8 * L], in0=Db[0:L, 0:8 * L], scalar1=0.0, scalar2=None, op0=ALU.min),
                [r_Db], [r_Db])
            if has_state:
                act(eA[:, 0:8 * L], psA[:, 0:8 * L], AF.Exp, [rpsA], [r_eA])
            yield
            act(Eb[0:L, 0:8 * L], Db[0:L, 0:8 * L], AF.Exp, [r_Db], [r_Eb])
            if has_state:
                vtt(v3(Cp, 128), v3(eA, 128), CT[:, g, cols].unsqueeze(1).to_broadcast([128, 8, L]), ALU.mult,
                    [r_eA, r_CT[g]], [r_Cp])
            yield
            vtt(v3(MT), v3(Eb), CBm[0:L, 0:L].unsqueeze(1).to_broadcast([L, 8, L]), ALU.mult, [r_Eb, r_CBm], [r_MT])
            pst, rpst = next_trb()
            S.group("pe", [tr(pst[0:L, cc * 128:(cc + 1) * 128], br[:, 4 * g + cc, cols], identb[:]) for cc in range(4)],
                    r_br[4 * g:4 * g + 4] + [r_c], [rpst])
            acopy(xt[0:L, :], pst[0:L, :], [rpst], [r_xt])
            yield
            vtt(x3(xd), x3(xt), dtt[0:L, q, 8 * g:8 * g + 8].unsqueeze(2).to_broadcast([L, 8, 64]), ALU.mult, [r_xt, r_dt], [r_xd])
            vtt(x3(xs2), x3(xt), wdt[0:L, 8 * g:8 * g + 8].unsqueeze(2).to_broadcast([L, 8, 64]), ALU.mult, [r_xt, r_ch], [r_xs2])
            pstB, rpstB = next_trb()
            S.group("pe", [tr(pstB[0:L, 0:128], BT[:, g, cols], identb[:])], [r_BT[g], r_c], [rpstB])
            acopy(Btok[0:L, :], pstB[0:L, 0:128], [rpstB], [r_Btok])
            yield
            psY, rpsY = chain_aux(slot)
            fns = []
            for h in range(8):
                fns.append(mm(psY[0:L, 64 * h:64 * h + 64], MT[0:L, h * L:(h + 1) * L], xd[0:L, 64 * h:64 * h + 64], True, not has_state))
                if has_state:
                    hh = 8 * g + h
                    fns.append(mm(psY[0:L, 64 * h:64 * h + 64], Cp[:, h * L:(h + 1) * L], hTb[:, hh * 64:hh * 64 + 64], False, True))
            S.group("pe", fns, [r_MT, r_xd] + ([r_Cp, r_hTb[g]] if has_state else []), [rpsY])
            psH, rpsH = chain_aux(slot)
            S.group("pe", [mm(psH[:, :], Btok[0:L, :], xs2[0:L, :])], [r_Btok, r_xs2], [rpsH])
            yield
            acopy(ytb[0:L, :], psY[0:L, :], [rpsY], [r_ytb])
            hg = hT[:, g * 512:(g + 1) * 512]
            if has_state:
                vtt(hg.rearrange("p (h d) -> p h d", h=8), hg.rearrange("p (h d) -> p h d", h=8),
                    etot[:, 8 * g:8 * g + 8].unsqueeze(2).to_broadcast([128, 8, 64]), ALU.mult, [r_hT[g], r_ch], [r_hT[g]])
                vtt(hg, hg, psH[:, :], ALU.add, [r_hT[g], rpsH], [r_hT[g]])
            else:
                vcopy(hg, psH[:, :], [rpsH], [r_hT[g]])
            yield
            acopy(hTb[:, g * 512:(g + 1) * 512], hg, [r_hT[g]], [r_hTb[g]])
            pst2, rpst2 = next_trb()
            S.group("pe", [tr(pst2[:, cc * L:(cc + 1) * L], ytb[0:L, cc * 128:(cc + 1) * 128], identb[0:L, 0:L]) for cc in range(4)],
                    [r_ytb, r_c], [rpst2])
            for cc in range(4):
                c = 4 * g + cc
                vop(lambda c=c, cc=cc: V.scalar_tensor_tensor(out=br[:, c, cols], in0=br[:, c, cols], scalar=dcol[:, c:c + 1],
                                                              in1=pst2[:, cc * L:(cc + 1) * L], op0=ALU.mult, op1=ALU.add),
                    [r_br[c], r_par, rpst2], [r_br[c]])
            yield

        def run_chains(makers, width=2):
            active = {}
            nxt = 0
            while nxt < len(makers) or active:
                for s in range(width):
                    if s not in active and nxt < len(makers):
                        active[s] = makers[nxt](s)
                        nxt += 1
                        try:
                            next(active[s])
                        except StopIteration:
                            del active[s]
                for s in list(active.keys()):
                    try:
                        next(active[s])
                    except StopIteration:
                        del active[s]

        def ssd_chunk(q, c0, L, has_state):
            run_chains([(lambda slot, g=g: ssd_gchain(q, c0, L, has_state, g, slot)) for g in range(4)], width=2)

        def ssd_state_out(dst):
            for qd in range(4):
                ps, rps = next_aux()
                S.group("pe", [tr(ps[:, j * 128:(j + 1) * 128], hT[:, (4 * qd + j) * 128:(4 * qd + j + 1) * 128], ident[:]) for j in range(4)],
                        r_hT + [r_c], [rps])
                vcopy(stg[:, qd * 512:(qd + 1) * 512], ps[:, :], [rps], [r_stg])
            S.dma("qsp", dst.rearrange("(c p) n -> p c n", p=128), stg[:].rearrange("p (c n) -> p c n", c=16), reads=[r_stg])

        def ssd_core():
            if seg > 0:
                S.dma("qsp", hT[:], hscr[l], reads=[r_hscr[l]], writes=r_hT)
                acopy(hTb[:], hT[:], r_hT, r_hTb)
            run_chains([(lambda slot, q=q, c0=c0, L=L, g=g: ssd_gchain(q, c0, L, not (seg == 0 and q == 0), g, slot))
                        for q, (c0, L) in enumerate(pchunks) for g in range(4)], width=2)
            if last_seg:
                ssd_state_out(ssm_p[l])
                for r in range(3):
                    S.dma("qsp", conv_p[l, r].rearrange("(c p) -> p c", p=128), convc[:, l, :, r], reads=[r_convc[l]],
                          allow_slow_non_contiguous=True)
            else:
                S.dma("qsp", hscr[l], hT[:], reads=r_hT, writes=[r_hscr[l]])
            for i, (c0, L) in enumerate(schunks):
                S.dma("qsp", stg[:].rearrange("p (c n) -> p c n", c=16), sssm_d[l, i].rearrange("h p n -> (h p) n").rearrange("(c p) n -> p c n", p=128),
                      writes=[r_stg])
                for qd in range(4):
                    ps, rps = next_aux()
                    S.group("pe", [tr(ps[:, j * 128:(j + 1) * 128], stg[:, (4 * qd + j) * 128:(4 * qd + j + 1) * 128], ident[:]) for j in range(4)],
                            [r_stg, r_c], [rps])
                    vcopy(hT[:, qd * 512:(qd + 1) * 512], ps[:, :], [rps], r_hT)
                acopy(hTb[:], hT[:], r_hT, r_hTb)
                ssd_chunk(8 + i, c0, L, True)
                ssd_state_out(ssm_s[l, i])
            if has_s:
                for b in range(4):
                    for r in range(3):
                        S.dma("qsp", conv_s[l, b, r].rearrange("(c p) -> p c", p=128), hso[:, :, b, r], reads=[r_hso],
                              allow_slow_non_contiguous=True)
        J(ssd_core)
        J(S.barrier)
        marks["ssd"] = len(jobs)

        for j in range(4):
            def zcons(cc, outs, j=j):
                c = 4 * j + cc
                for (t0, tn, ps, rps) in outs:
                    i = rot["pjs"] % 2
                    act(szt[i][:, 0:tn], ps[:, 0:tn], AF.Silu, [rps], [r_szt[i]])
                    vtt(br[:, c, t0:t0 + tn], br[:, c, t0:t0 + tn], szt[i][:, 0:tn], ALU.mult, [r_br[c], r_szt[i]], [r_br[c]])
                    rot["pjs"] += 0
            proj_job(jobs, w_in_v, Z0 + 512 * j, 512, hnT, r_hn, tiles, zcons)

        def ssd_norm():
            for g in range(4):
                sumsq_rstd([br[:, 4 * g + cc, 0:ncol] for cc in range(4)], mean512, rstd1, tsq1, rts1, r_br[4 * g:4 * g + 4])
                for cc in range(4):
                    c = 4 * g + cc
                    vop(lambda c=c: V.scalar_tensor_tensor(out=br[:, c, 0:ncol], in0=br[:, c, 0:ncol], scalar=nw[:, c:c + 1],
                                                           in1=rstd1[:, 0:ncol], op0=ALU.mult, op1=ALU.mult),
                        [r_br[c], r_par, r_rstd], [r_br[c]])
        J(ssd_norm)
        marks["ssdnorm"] = len(jobs)
        branch_jobs(cv1b, w_br_ssm_d, GA0, True)
        J(S.barrier)
        marks["p1"] = len(jobs)

        cv2 = Carver()
        rope = cv2.F(4 * 576).rearrange("p (t c) -> p t c", t=4); r_rope = R()
        Sf = cv2.F(2048); r_Sf = [R() for _ in range(8)]
        t1 = [cv2.F(576), cv2.F(576)]; r_t1 = [R(), R()]
        t2 = [cv2.F(576), cv2.F(576)]; r_t2 = [R(), R()]
        qT = cv2.B(8 * 576).rearrange("p (h t) -> p h t", h=8); r_qT = [R() for _ in range(8)]
        kT = cv2.B(8 * 576).rearrange("p (h t) -> p h t", h=8); r_kT = [R() for _ in range(8)]
        ub = [cv2.B(576), cv2.B(576)]; r_ub = [R(), R()]
        Sb = cv2.B(2048); r_Sb = [R() for _ in range(8)]
        NRS = 4
        PT_s = [cv2.B(64, 64) for _ in range(NRS)]; r_PT_s = [R() for _ in range(NRS)]
        vt_s = [cv2.B(256, 64) for _ in range(NRS)]; r_vt_s = [R() for _ in range(NRS)]
        kt_s = [cv2.B(128, 64) for _ in range(NRS)]; r_kt_s = [R() for _ in range(NRS)]
        qp_s = [cv2.B(64) for _ in range(NRS)]; r_qp_s = [R() for _ in range(NRS)]
        rsel = dict(i=0)
        cv2b = Carver()
        rstd2 = cv2b.F(576)
        tsq2 = [cv2b.B(576), cv2b.B(576)]; rts2 = [R(), R()]
        sgt = [cv2b.B(576), cv2b.B(576)]; r_sgt = [R(), R()]

        def ret_pre():
            S.dma("qsp", rope, rope_d[seg].rearrange("t p c -> p t c"), writes=[r_rope])
        J(ret_pre)

        def rot_cons(dstT, r_dst, tc, ts, j):
            def cons(cc, outs):
                h = 4 * j + cc
                for (t0, tn, ps, rps) in outs:
                    i = rot["trb"] % 2
                    rot["trb"] += 1
                    acopy(ub[i][:, 0:tn], ps[:, 0:tn], [rps], [r_ub[i]])
                    ps2, rps2 = ps_for(tn)
                    S.group("pe", [mm(ps2[:, 0:tn], pswapb[:], ub[i][:, 0:tn])], [r_ub[i], r_c], [rps2])
                    vtt(t1[i][:, 0:tn], ps[:, 0:tn], rope[:, tc, t0:t0 + tn], ALU.mult, [rps, r_rope], [r_t1[i]])
                    vtt(t2[i][:, 0:tn], ps2[:, 0:tn], rope[:, ts, t0:t0 + tn], ALU.mult, [rps2, r_rope], [r_t2[i]])
                    vtt(dstT[:, h, t0:t0 + tn], t1[i][:, 0:tn], t2[i][:, 0:tn], ALU.add, [r_t1[i], r_t2[i]], [r_dst[h]])
            return cons
        for j in range(2):
            proj_job(jobs, w_in_v, RQ0 + 512 * j, 512, hnT, r_hn, tiles, rot_cons(qT, r_qT, 0, 1, j))
        for j in range(2):
            proj_job(jobs, w_in_v, RK0 + 512 * j, 512, hnT, r_hn, tiles, rot_cons(kT, r_kT, 2, 3, j))
        for j in range(4):
            def vcons(cc, outs, j=j):
                c = 4 * j + cc
                for (t0, tn, ps, rps) in outs:
                    acopy(br[:, c, t0:t0 + tn], ps[:, 0:tn], [rps], [r_br[c]])
            proj_job(jobs, w_in_v, RV0 + 512 * j, 512, hnT, r_hn, tiles, vcons)

        def ret_gchain(h, c0, L, has_state, slot):
            cols = slice(c0, c0 + L)
            PT, vt, kt, qp = PT_s[slot], vt_s[slot], kt_s[slot], qp_s[slot]
            r_PT, r_vt, r_kt, r_qp = r_PT_s[slot], r_vt_s[slot], r_kt_s[slot], r_qp_s[slot]
            pss, rpss = chain_aux(slot)
            S.group("pe", [mm(pss[0:L, 0:L], kT[:, h, cols], qT[:, h, cols])], [r_kT[h], r_qT[h]], [rpss])
            if has_state:
                vtt(qp[:, 0:L], qT[:, h, cols], g1[:, h * 64:h * 64 + L], ALU.mult, [r_qT[h], r_c], [r_qp])
            yield
            vtt(PT[0:L, 0:L], pss[0:L, 0:L], dm[0:L, h * 64:h * 64 + L], ALU.mult, [rpss, r_c], [r_PT])
            pst, rpst = next_trb()
            S.group("pe", [tr(pst[0:L, 0:128], br[:, 2 * h, cols], identb[:]), tr(pst[0:L, 128:256], br[:, 2 * h + 1, cols], identb[:]),
                           tr(pst[0:L, 256:384], kT[:, h, cols], identb[:])], [r_br[2 * h], r_br[2 * h + 1], r_kT[h], r_c], [rpst])
            acopy(vt[0:L, :], pst[0:L, 0:256], [rpst], [r_vt])
            tcol = h if L == 64 else 8 + h
            vop(lambda: V.tensor_scalar(out=kt[0:L, :], in0=pst[0:L, 256:384], scalar1=te[0:L, tcol:tcol + 1], scalar2=None, op0=ALU.mult),
                [rpst, r_c], [r_kt])
            yield
            pso, rpso = chain_aux(slot)
            fns = []
            for vv in range(2):
                fns.append(mm(pso[:, vv * L:(vv + 1) * L], vt[0:L, vv * 128:(vv + 1) * 128], PT[0:L, 0:L], True, not has_state))
                if has_state:
                    fns.append(mm(pso[:, vv * L:(vv + 1) * L], Sb[:, h * 256 + vv * 128:h * 256 + vv * 128 + 128], qp[:, 0:L], False, True))
            S.group("pe", fns, [r_vt, r_PT] + ([r_Sb[h], r_qp] if has_state else []), [rpso])
            psS, rpsS = chain_aux(slot)
            S.group("pe", [mm(psS[:, 0:256], kt[0:L, :], vt[0:L, :])], [r_kt, r_vt], [rpsS])
            yield
            for vv in range(2):
                acopy(br[:, 2 * h + vv, cols], pso[:, vv * L:(vv + 1) * L], [rpso], [r_br[2 * h + vv]])
            sh = Sf[:, h * 256:(h + 1) * 256]
            if has_state:
                gl = math.exp(LAM[h] * L)
                vop(lambda: V.scalar_tensor_tensor(out=sh, in0=sh, scalar=gl, in1=psS[:, 0:256], op0=ALU.mult, op1=ALU.add),
                    [r_Sf[h], rpsS], [r_Sf[h]])
            else:
                vcopy(sh, psS[:, 0:256], [rpsS], [r_Sf[h]])
            yield
            acopy(Sb[:, h * 256:(h + 1) * 256], sh, [r_Sf[h]], [r_Sb[h]])
            yield

        def ret_chunk(h, c0, L, has_state):
            for _ in ret_gchain(h, c0, L, has_state, 0):
                pass

        def ret_core():
            if seg > 0:
                S.dma("qsp", Sf[:], sscr[l], reads=[r_sscr[l]], writes=r_Sf)
                acopy(Sb[:], Sf[:], r_Sf, r_Sb)
            run_chains([(lambda slot, q=q, c0=c0, L=L, h=h: ret_gchain(h, c0, L, not (seg == 0 and q == 0), slot))
                        for q, (c0, L) in enumerate(pchunks) for h in range(8)], width=2)
            if last_seg:
                S.dma("qsp", ret_p[l].rearrange("h d v -> d h v"), Sf[:].rearrange("p (h v) -> p h v", h=8), reads=r_Sf)
            else:
                S.dma("qsp", sscr[l], Sf[:], reads=r_Sf, writes=[r_sscr[l]])
            for i, (c0, L) in enumerate(schunks):
                S.dma("qsp", Sf[:].rearrange("p (h v) -> p h v", h=8), sret_d[l, i].rearrange("h d v -> d h v"), writes=r_Sf)
                acopy(Sb[:], Sf[:], r_Sf, r_Sb)
                run_chains([(lambda slot, h=h, c0=c0, L=L: ret_gchain(h, c0, L, True, slot)) for h in range(8)], width=2)
                S.dma("qsp", ret_s[l, i].rearrange("h d v -> d h v"), Sf[:].rearrange("p (h v) -> p h v", h=8), reads=r_Sf)
        marks["retproj"] = len(jobs)
        J(ret_core)
        J(S.barrier)
        marks["retcore"] = len(jobs)

        def ret_post():
            for h in range(8):
                sumsq_rstd([br[:, 2 * h + vv, 0:ncol] for vv in range(2)], mean256, rstd2, tsq2, rts2, r_br[2 * h:2 * h + 2])
                for vv in range(2):
                    c = 2 * h + vv
                    vtt(br[:, c, 0:ncol], br[:, c, 0:ncol], rstd2[:, 0:ncol], ALU.mult, [r_br[c], r_rstd], [r_br[c]])
        J(ret_post)
        for j in range(4):
            def gcons2(cc, outs, j=j):
                c = 4 * j + cc
                for (t0, tn, ps, rps) in outs:
                    i = rot["trb"] % 2
                    rot["trb"] += 1
                    act(sgt[i][:, 0:tn], ps[:, 0:tn], AF.Silu, [rps], [r_sgt[i]])
                    vtt(br[:, c, t0:t0 + tn], br[:, c, t0:t0 + tn], sgt[i][:, 0:tn], ALU.mult, [r_br[c], r_sgt[i]], [r_br[c]])
            proj_job(jobs, w_in_v, RG0 + 512 * j, 512, hnT, r_hn, tiles, gcons2)
        branch_jobs(cv2b, w_br_ret_d, GB0, False)
        J(S.barrier)
        marks["p2"] = len(jobs)

        cv3 = Carver()
        _bt0 = cv3.F(576, 64); _rbt0 = R()
        bt = [_bt0, _bt0]; r_bt = [_rbt0, _rbt0]
        if has_s:
            _btr0 = cv3.F(576, 64); _rbtr0 = R()
            btr = [_btr0, _btr0]; r_btr = [_rbtr0, _rbtr0]
        else:
            btr = [cv3.F(576, 64), cv3.F(576, 64)]; r_btr = [R(), R()]
        NAB = 4
        tS2 = [cv3.F(512, 64) for _ in range(NAB)]; r_tS2 = [R() for _ in range(NAB)]
        tS = tS2[0]; r_tS = r_tS2[0]
        tB = cv3.F(64, 64); r_tB = R()
        rd = cv3.F(512); r_rd = R()
        _kf0 = cv3.F(576); _rkf0 = R()
        kf2 = [_kf0, _kf0]; r_kf2 = [_rkf0, _rkf0]
        vf = cv3.F(576); r_vf = R()
        ostg = cv3.F(512); r_ostg = R()
        rs32 = [cv3.F(576), cv3.F(576)]; r_rs32 = [R(), R()]
        nsel = dict(i=0)
        qaT = cv3.B(4 * 576).rearrange("p (h t) -> p h t", h=4); r_qa = [R() for _ in range(4)]
        kaT = cv3.B(4 * 576).rearrange("p (h t) -> p h t", h=4); r_ka = [R() for _ in range(4)]
        if seg > 0:
            khist2 = [cv3.B(512), cv3.B(512)]
            vhistT2 = [cv3.B(512), cv3.B(512)]
        else:
            khist2 = [None, None]
            vhistT2 = [None, None]
        r_khist2 = [R(), R()]; r_vhist2 = [R(), R()]
        _vt0 = cv3.B(16 * 128, 64).rearrange("p (k d) -> p k d", k=16); _rvt0 = R()
        vtok2 = [_vt0, _vt0]; r_vtok2 = [_rvt0, _rvt0]
        ktokh = cv3.B(8 * 128, 64).rearrange("p (k d) -> p k d", k=8); r_ktokh = R()
        vtokh = cv3.B(8 * 128, 64).rearrange("p (k d) -> p k d", k=8); r_vtokh = R()
        khT = cv3.B(512); r_khT = R()
        vnew = cv3.B(128, 64); r_vnew = R()
        PT2 = [cv3.B(512, 64) for _ in range(NAB)]; r_PT2 = [R() for _ in range(NAB)]
        PTa = PT2[0]; r_PTa = r_PT2[0]
        PTb = cv3.B(64, 64); r_PTb = R()
        samp = [dict(ktokh=ktokh, vtokh=vtokh, khT=khT, vnew=vnew, tS=tS2[0], tB=tB, PTa=PT2[0], PTb=PTb, rd=rd,
                     r=dict(ktokh=r_ktokh, vtokh=r_vtokh, khT=r_khT, vnew=r_vnew, tS=r_tS2[0], tB=r_tB, PTa=r_PT2[0], PTb=r_PTb, rd=r_rd),
                     banks=[(aux[0], r_aux[0]), (aux[1], r_aux[1]), (pj[0], r_pj[0])])]
        if has_s:
            samp.append(dict(ktokh=cv3.B(8 * 128, 64).rearrange("p (k d) -> p k d", k=8),
                             vtokh=cv3.B(8 * 128, 64).rearrange("p (k d) -> p k d", k=8),
                             khT=cv3.B(512), vnew=cv3.B(128, 64), tS=tS2[1], tB=cv3.F(64, 64), PTa=PT2[1], PTb=cv3.B(64, 64),
                             rd=cv3.F(16),
                             r=dict(ktokh=R(), vtokh=R(), khT=R(), vnew=R(), tS=r_tS2[1], tB=R(), PTa=r_PT2[1], PTb=R(), rd=R()),
                             banks=[(aux[2], r_aux[2]), (aux[3], r_aux[3]), (pj[1], r_pj[1])]))
        sq3 = [cv3.B(576), cv3.B(576)]; r_sq3 = [R(), R()]
        par = seg % 2

        def norm_cons(dstT, r_dst, wcol, keep_f32, j):
            def cons(cc, outs):
                kf, r_kf = kf2[nsel["i"] % 2], r_kf2[nsel["i"] % 2]
                nsel["i"] += 1
                for (t0, tn, ps, rps) in outs:
                    i = rot["trb"] % 2
                    rot["trb"] += 1
                    rs3, r_rs3 = rs32[i], r_rs32[i]
                    act(sq3[i][:, 0:tn], ps[:, 0:tn], AF.Square, [rps], [r_sq3[i]])
                    ps2, rps2 = ps_for(tn)
                    S.group("pe", [mm(ps2[:, 0:tn], mean128[:], sq3[i][:, 0:tn])], [r_sq3[i], r_c], [rps2])
                    act(rs3[:, 0:tn], ps2[:, 0:tn], AF.Ln, [rps2], [r_rs3], bias=EPS, scale=1.0)
                    act(rs3[:, 0:tn], rs3[:, 0:tn], AF.Exp, [r_rs3], [r_rs3], scale=-0.5)
                    if keep_f32:
                        vop(lambda: V.scalar_tensor_tensor(out=kf[:, t0:t0 + tn], in0=ps[:, 0:tn], scalar=wcol[:, 0:1], in1=rs3[:, 0:tn],
                                                           op0=ALU.mult, op1=ALU.mult), [rps, r_par, r_rs3], [r_kf])
                        acopy(dstT[:, cc, t0:t0 + tn], kf[:, t0:t0 + tn], [r_kf], [r_dst[cc]])
                    else:
                        vop(lambda: V.scalar_tensor_tensor(out=dstT[:, cc, t0:t0 + tn], in0=ps[:, 0:tn], scalar=wcol[:, 0:1], in1=rs3[:, 0:tn],
                                                           op0=ALU.mult, op1=ALU.mult), [rps, r_par, r_rs3], [r_dst[cc]])
                if keep_f32:
                    kv_out(kf, r_kf, k_p, k_s, 4 * j + cc)
            return cons

        def kv_out(src, r_src, dst_p, dst_s, h):
            if last_seg:
                ps, rps = next_aux()
                S.group("pe", [tr(ps[:, tt * 128:(tt + 1) * 128], src[:, tt * 128:(tt + 1) * 128], ident[:]) for tt in range(4)],
                        [r_src, r_c], [rps])
                vcopy(ostg[:, :], ps[:, :], [rps], [r_ostg])
                S.dma("qsp", dst_p[l].rearrange("(tt p) h d -> p tt h d", p=128)[:, :, h, :], ostg[:, :].rearrange("p (tt d) -> p tt d", tt=4),
                      reads=[r_ostg])
            if has_s:
                ps, rps = next_aux()
                S.group("pe", [tr(ps[0:64, 0:128], src[:, 512:576], ident[:])], [r_src, r_c], [rps])
                vcopy(ostg[0:64, 0:128], ps[0:64, 0:128], [rps], [r_ostg])
                S.dma("qsp", dst_s[l][:, h, :], ostg[0:64, 0:128], reads=[r_ostg])

        def attend(qap, Lq, keys, bb0, outap, r_q, r_keys, r_out, ib):
            nblk = len(keys)
            n8 = min(nblk, 8)
            psA, rpsA = next_aux()
            psB, rpsB = next_aux()
            fns = []
            for n, (kap, vap, nk) in enumerate(keys):
                tgt = psA[0:nk, n * Lq:(n + 1) * Lq] if n < 8 else psB[0:nk, 0:Lq]
                fns.append(mm(tgt, kap, qap))
            S.group("pe", fns, [r_q] + r_keys, [rpsA, rpsB])
            btv = bt[ib].rearrange("p (b i) -> p b i", b=9)
            vop(lambda: V.scalar_tensor_tensor(out=tS[:, 0:n8 * Lq].rearrange("p (n i) -> p n i", n=n8),
                                               in0=psA[0:64, 0:n8 * Lq].rearrange("p (n i) -> p n i", n=n8), scalar=ATT_SCALE,
                                               in1=btv[:, bb0:bb0 + n8, 0:Lq], op0=ALU.mult, op1=ALU.add),
                [rpsA, r_bt[ib]], [r_tS])
            act(PTa[:, 0:n8 * Lq], tS[:, 0:n8 * Lq], AF.Exp, [r_tS], [r_PTa])
            if nblk == 9:
                nk = keys[8][2]
                vop(lambda: V.scalar_tensor_tensor(out=tB[0:nk, 0:Lq], in0=psB[0:nk, 0:Lq], scalar=ATT_SCALE,
                                                   in1=btv[0:nk, bb0 + 8, 0:Lq], op0=ALU.mult, op1=ALU.add),
                    [rpsB, r_bt[ib]], [r_tB])
                act(PTb[0:nk, 0:Lq], tB[0:nk, 0:Lq], AF.Exp, [r_tB], [r_PTb])
            psO, rpsO = next_aux()
            fns = []
            for n, (kap, vap, nk) in enumerate(keys):
                pt = PTa[0:nk, n * Lq:(n + 1) * Lq] if n < 8 else PTb[0:nk, 0:Lq]
                fns.append(mm(psO[:, 0:Lq], vap, pt, n == 0, n == nblk - 1))
            for n, (kap, vap, nk) in enumerate(keys):
                pt = PTa[0:nk, n * Lq:(n + 1) * Lq] if n < 8 else PTb[0:nk, 0:Lq]
                fns.append(mm(psO[:, 64:64 + Lq], ones64b[0:nk, :], pt, n == 0, n == nblk - 1))
            S.group("pe", fns, [r_PTa, r_PTb, r_c] + r_keys, [rpsO])
            act(rd[:, 0:Lq], psO[:, 64:64 + Lq], AF.Ln, [rpsO], [r_rd])
            act(rd[:, 0:Lq], rd[:, 0:Lq], AF.Exp, [r_rd], [r_rd], scale=-1.0)
            vtt(outap, psO[:, 0:Lq], rd[:, 0:Lq], ALU.mult, [rpsO, r_rd], [r_out])

        def att_heads(j):
            for cc in range(4):
                h = 4 * j + cc
                ib = h % 2
                S.dma("qsp", btr[ib].rearrange("p (b i) -> p b i", b=9), biasr_d[l, h].rearrange("(b jj) i -> jj b i", jj=64),
                      writes=[r_btr[ib]])
                if has_s:
                    S.dma("qsp", bt[ib].rearrange("p (b i) -> p b i", b=9), bias_d[l, h].rearrange("(b jj) i -> jj b i", jj=64),
                          writes=[r_bt[ib]])
                kcs0 = 0 if seg > 0 else 8
                khist, vhistT, vtok = khist2[ib], vhistT2[ib], vtok2[ib]
                r_khist, r_vhist, r_vtok = r_khist2[ib], r_vhist2[ib], r_vtok2[ib]
                if seg > 0:
                    S.dma("qsp", khist[:, :], khd[l, 1 - par][:, h, :], reads=[r_khd[l][1 - par][j]], writes=[r_khist])
                    S.dma("qsp", vhistT[:, :], vhd[l, 1 - par][:, h, :], reads=[r_vhd[l][1 - par][j]], writes=[r_vhist])
                for k4 in range(kcs0 // 4, 4):
                    pst, rpst = next_trb()
                    fns = []
                    for jj in range(4):
                        kc = 4 * k4 + jj
                        src = vhistT[:, kc * 64:(kc + 1) * 64] if kc < 8 else br[:, h, (kc - 8) * 64:(kc - 7) * 64]
                        fns.append(tr(pst[0:64, jj * 128:(jj + 1) * 128], src, identb[:]))
                    S.group("pe", fns, [r_vhist, r_br[h], r_c], [rpst])
                    acopy(vtok[:, 4 * k4:4 * k4 + 4, :], pst[0:64, :].rearrange("p (k d) -> p k d", k=4), [rpst], [r_vtok])
                kcs = list(range(kcs0, 16))
                psO, rO = aux[2], r_aux[2]
                psD, rD = aux[3], r_aux[3]
                btrv = btr[ib].rearrange("p (b i) -> p b i", b=9)

                def rng(kc):
                    c_lo = max(0, kc - 8)
                    c_hi = min(7, kc)
                    return c_lo, c_hi, c_hi - c_lo + 1

                sbank = [(aux[0], r_aux[0]), (aux[1], r_aux[1]), (pj[0], r_pj[0]), (pj[1], r_pj[1])]
                LOOK = 3

                def emitS(idx, cc=cc, khist=khist, r_khist=r_khist):
                    kc = kcs[idx]
                    c_lo, c_hi, n = rng(kc)
                    sb_, rsb_ = sbank[idx % NAB]
                    kap = khist[:, kc * 64:(kc + 1) * 64] if kc < 8 else kaT[:, cc, (kc - 8) * 64:(kc - 7) * 64]
                    S.group("pe", [mm(sb_[0:64, 0:n * 64], kap, qaT[:, cc, c_lo * 64:(c_hi + 1) * 64])],
                            [r_khist, r_ka[cc], r_qa[cc]], [rsb_])
                for idx in range(min(LOOK, len(kcs))):
                    emitS(idx)
                for idx, kc in enumerate(kcs):
                    if idx + LOOK < len(kcs):
                        emitS(idx + LOOK)
                    c_lo, c_hi, n = rng(kc)
                    p = idx % NAB
                    sb_, rsb_ = sbank[p]
                    r_lo = 8 - kc + c_lo
                    vop(lambda p=p, n=n, r_lo=r_lo, sb_=sb_: V.scalar_tensor_tensor(
                        out=tS2[p][:, 0:n * 64].rearrange("p (n i) -> p n i", n=n),
                        in0=sb_[0:64, 0:n * 64].rearrange("p (n i) -> p n i", n=n), scalar=ATT_SCALE,
                        in1=btrv[:, r_lo:r_lo + n, :], op0=ALU.mult, op1=ALU.add), [rsb_, r_btr[ib]], [r_tS2[p]])
                    act(PT2[p][:, 0:n * 64], tS2[p][:, 0:n * 64], AF.Exp, [r_tS2[p]], [r_PT2[p]])
                    first = idx == 0
                    last = idx == len(kcs) - 1
                    S.group("pe", [mm(psO[:, c_lo * 64:(c_hi + 1) * 64], vtok[:, kc, :], PT2[p][:, 0:n * 64], first, last, skip=True),
                                   mm(psD[:, c_lo * 64:(c_hi + 1) * 64], ones64b[:, :], PT2[p][:, 0:n * 64], first, last, skip=True)],
                            [r_PT2[p], r_vtok, r_c], [rO, rD])
                act(rd[:, 0:512], psD[:, 0:512], AF.Ln, [rD], [r_rd])
                act(rd[:, 0:512], rd[:, 0:512], AF.Exp, [r_rd], [r_rd], scale=-1.0)
                vtt(br[:, h, 0:512], psO[:, 0:512], rd[:, 0:512], ALU.mult, [rO, r_rd], [r_br[h]])
                def samp_chain(i, c0, L, slot, cc=cc, h=h, ib=ib):
                    sbf = samp[slot]
                    rr = sbf["r"]
                    ktokh_, vtokh_, khT_, vnew_ = sbf["ktokh"], sbf["vtokh"], sbf["khT"], sbf["vnew"]
                    S.dma("qpool", ktokh_, ck_d[l, i][:, h, :].rearrange("(b jj) d -> jj b d", jj=64), writes=[rr["ktokh"]])
                    S.dma("qpool", vtokh_, cv_d[l, i][:, h, :].rearrange("(b jj) d -> jj b d", jj=64), writes=[rr["vtokh"]])
                    yield
                    pst, rpst = next_trb()
                    S.group("pe", [tr(pst[:, b_ * 64:(b_ + 1) * 64], ktokh_[:, b_, :], identb[0:64, 0:64]) for b_ in range(8)],
                            [rr["ktokh"], r_c], [rpst])
                    acopy(khT_[:, :], pst[:, :], [rpst], [rr["khT"]])
                    yield
                    pst2, rpst2 = next_trb()
                    S.group("pe", [tr(pst2[0:L, 0:128], br[:, h, c0:c0 + L], identb[:])], [r_br[h], r_c], [rpst2])
                    acopy(vnew_[0:L, :], pst2[0:L, 0:128], [rpst2], [rr["vnew"]])
                    yield
                    keys = [(khT_[:, b_ * 64:(b_ + 1) * 64], vtokh_[:, b_, :], 64) for b_ in range(8)]
                    keys.append((kaT[:, cc, c0:c0 + L], vnew_[0:L, :], L))
                    qap = qaT[:, cc, c0:c0 + L]
                    Lq = L
                    (psA, rpsA), (psB, rpsB), (psO, rpsO) = sbf["banks"]
                    tS_, tB_, PTa_, PTb_, rd_ = sbf["tS"], sbf["tB"], sbf["PTa"], sbf["PTb"], sbf["rd"]
                    r_keys = [rr["khT"], rr["vtokh"], r_ka[cc], rr["vnew"]]
                    fns = []
                    for n, (kap, vap, nk) in enumerate(keys):
                        tgt = psA[0:nk, n * Lq:(n + 1) * Lq] if n < 8 else psB[0:nk, 0:Lq]
                        fns.append(mm(tgt, kap, qap))
                    S.group("pe", fns, [r_qa[cc]] + r_keys, [rpsA, rpsB])
                    yield
                    btv = bt[ib].rearrange("p (b i) -> p b i", b=9)
                    vop(lambda: V.scalar_tensor_tensor(out=tS_[:, 0:8 * Lq].rearrange("p (n i) -> p n i", n=8),
                                                       in0=psA[0:64, 0:8 * Lq].rearrange("p (n i) -> p n i", n=8), scalar=ATT_SCALE,
                                                       in1=btv[:, 0:8, 0:Lq], op0=ALU.mult, op1=ALU.add),
                        [rpsA, r_bt[ib]], [rr["tS"]])
                    nk9 = keys[8][2]
                    vop(lambda: V.scalar_tensor_tensor(out=tB_[0:nk9, 0:Lq], in0=psB[0:nk9, 0:Lq], scalar=ATT_SCALE,
                                                       in1=btv[0:nk9, 8, 0:Lq], op0=ALU.mult, op1=ALU.add),
                        [rpsB, r_bt[ib]], [rr["tB"]])
                    yield
                    act(PTa_[:, 0:8 * Lq], tS_[:, 0:8 * Lq], AF.Exp, [rr["tS"]], [rr["PTa"]])
                    act(PTb_[0:nk9, 0:Lq], tB_[0:nk9, 0:Lq], AF.Exp, [rr["tB"]], [rr["PTb"]])
                    yield
                    fns = []
                    for n, (kap, vap, nk) in enumerate(keys):
                        pt = PTa_[0:nk, n * Lq:(n + 1) * Lq] if n < 8 else PTb_[0:nk, 0:Lq]
                        fns.append(mm(psO[:, 0:Lq], vap, pt, n == 0, n == 8))
                    for n, (kap, vap, nk) in enumerate(keys):
                        pt = PTa_[0:nk, n * Lq:(n + 1) * Lq] if n < 8 else PTb_[0:nk, 0:Lq]
                        fns.append(mm(psO[:, 64:64 + Lq], ones64b[0:nk, :], pt, n == 0, n == 8))
                    S.group("pe", fns, [rr["PTa"], rr["PTb"], r_c] + r_keys, [rpsO])
                    yield
                    act(rd_[:, 0:Lq], psO[:, 64:64 + Lq], AF.Ln, [rpsO], [rr["rd"]])
                    yield
                    act(rd_[:, 0:Lq], rd_[:, 0:Lq], AF.Exp, [rr["rd"]], [rr["rd"]], scale=-1.0)
                    yield
                    vtt(br[:, h, c0:c0 + L], psO[:, 0:Lq], rd_[:, 0:Lq], ALU.mult, [rpsO, rr["rd"]], [r_br[h]])
                    yield
                if schunks:
                    run_chains([(lambda slot, i=i, c0=c0, L=L: samp_chain(i, c0, L, slot)) for i, (c0, L) in enumerate(schunks)], width=2)

        for j in range(4):
            proj_job(jobs, w_in_v, AQ0 + 512 * j, 512, hnT, r_hn, tiles, norm_cons(qaT, r_qa, qn, False, j))
            proj_job(jobs, w_in_v, AK0 + 512 * j, 512, hnT, r_hn, tiles, norm_cons(kaT, r_ka, kn, True, j))

            def avcons(cc, outs, j=j):
                h = 4 * j + cc
                for (t0, tn, ps, rps) in outs:
                    acopy(vf[:, t0:t0 + tn], ps[:, 0:tn], [rps], [r_vf])
                    vcopy(br[:, h, t0:t0 + tn], ps[:, 0:tn], [rps], [r_br[h]])
                kv_out(vf, r_vf, v_p, v_s, h)
            proj_job(jobs, w_in_v, AV0 + 512 * j, 512, hnT, r_hn, tiles, avcons)

            def hist_save(j=j):
                if not last_seg:
                    S.dma("qsp", khd[l, par][:, 4 * j:4 * j + 4, :], kaT[:, :, 0:512], reads=r_ka, writes=[r_khd[l][par][j]])
                    S.dma("qsp", vhd[l, par][:, 4 * j:4 * j + 4, :], br[:, 4 * j:4 * j + 4, 0:512], reads=r_br[4 * j:4 * j + 4],
                          writes=[r_vhd[l][par][j]])
                att_heads(j)
            J(hist_save)
        J(S.barrier)
        marks["att"] = len(jobs)
        branch_jobs(Carver(), w_br_att_d, GC0, False)
        J(S.barrier)
        marks["p3"] = len(jobs)

        wov = wview(w_out_d[l])
        for j in range(4):
            def ocons(cc, outs, j=j):
                c = 4 * j + cc
                for (t0, tn, ps, rps) in outs:
                    vtt(xT[:, c, t0:t0 + tn], xT[:, c, t0:t0 + tn], ps[:, 0:tn], ALU.add, [r_x[c], rps], [r_x[c]])
            proj_job(jobs, wov, 512 * j, 512, mT, r_m, tiles, ocons)

        cv4 = Carver()
        rstd4 = cv4.F(576)
        tsq4 = [cv4.B(576), cv4.B(576)]; rts4 = [R(), R()]
        sa = [cv4.B(4 * 576).rearrange("p (c t) -> p c t", c=4) for _ in range(2)]; r_sa = [R(), R()]
        actb = [cv4.B(4 * 576).rearrange("p (c t) -> p c t", c=4) for _ in range(2)]; r_actb = [R(), R()]
        yst = cv4.F(2048); r_yst = R()

        def norm2():
            sumsq_rstd([xT[:, c, 0:ncol] for c in range(16)], meanD, rstd4, tsq4, rts4, r_x)
            for c in range(16):
                vop(lambda c=c: V.scalar_tensor_tensor(out=hnT[:, c, 0:ncol], in0=xT[:, c, 0:ncol], scalar=gf[:, c:c + 1],
                                                       in1=rstd4[:, 0:ncol], op0=ALU.mult, op1=ALU.mult),
                    [r_x[c], r_par, r_rstd], [r_hn[c]])
        J(norm2)
        wfv = wview(w_ffn_in_d[l])
        wfo = w_ffn_out_d[l].rearrange("(kc p) c -> p kc c", p=128)
        for j in range(11):
            i = j % 2

            def acons(cc, outs, i=i):
                for (t0, tn, ps, rps) in outs:
                    act(sa[i][:, cc, t0:t0 + tn], ps[:, 0:tn], AF.Silu, [rps], [r_sa[i]])
            proj_job(jobs, wfv, 512 * j, 512, hnT, r_hn, tiles, acons)

            def ccons(cc, outs, i=i):
                for (t0, tn, ps, rps) in outs:
                    vtt(actb[i][:, cc, t0:t0 + tn], ps[:, 0:tn], sa[i][:, cc, t0:t0 + tn], ALU.mult, [rps, r_sa[i]], [r_actb[i]])
            proj_job(jobs, wfv, FFH + 512 * j, 512, hnT, r_hn, tiles, ccons)

            def ld(slot, j=j):
                sv = slabs[slot][:, :].rearrange("p (k c) -> p k c", k=4)
                S.dma("qpool", sv, wfo[:, 4 * j:4 * j + 4, :], writes=[r_slab[slot]])

            def fn(slot, i=i):
                sv = slabs[slot][:, :].rearrange("p (k c) -> p k c", k=4)
                for oc in range(16):
                    for (t0, tn) in tiles:
                        ps, rps = ps_for(tn)
                        S.group("pe", [mm(ps[:, 0:tn], sv[:, k, oc * 128:(oc + 1) * 128], actb[i][:, k, t0:t0 + tn], k == 0, k == 3)
                                       for k in range(4)], [r_slab[slot], r_actb[i]], [rps])
                        vtt(xT[:, oc, t0:t0 + tn], xT[:, oc, t0:t0 + tn], ps[:, 0:tn], ALU.add, [r_x[oc], rps], [r_x[oc]])
            jobs.append((ld, fn))

        def y_out():
            dsts = [(y_p[seg * 512 + 128 * tt: seg * 512 + 128 * tt + 128, :], 128, 128 * tt) for tt in range(4)]
            if has_s:
                dsts.append((y_s[:, :], 64, 512))
            for (dst, nt, c0) in dsts:
                for q in range(4):
                    ps, rps = next_aux()
                    S.group("pe", [tr(ps[0:nt, jx * 128:(jx + 1) * 128], xT[:, 4 * q + jx, c0:c0 + nt], ident[:]) for jx in range(4)],
                            r_x[4 * q:4 * q + 4] + [r_c], [rps])
                    vcopy(yst[0:nt, q * 512:(q + 1) * 512], ps[0:nt, :], [rps], [r_yst])
                S.dma("qsp", dst, yst[0:nt, :], reads=[r_yst])
        if last_layer:
            J(y_out)
        J(S.barrier)
        stop = os.environ.get("KSTOP")
        if stop:
            jobs = jobs[:marks[stop]]
            jobs.append((None, lambda _s: S.barrier()))
        return jobs

    all_jobs = []
    for seg in range(n_segs):
        for l in range(n_layers):
            all_jobs.extend(layer_pass(l, seg, l == n_layers - 1))
    run_jobs(all_jobs)
    S.finish()
    return nc


def _consts(rel_bias):
    f32 = np.float32
    half = 64
    inv = (np.float32(10000.0) ** (-np.arange(half, dtype=f32) * f32(2.0) / f32(128))).astype(f32)
    rope = np.zeros((4, 4, 128, 576), f32)
    for seg in range(4):
        pos = np.concatenate([seg * 512 + np.arange(512), 1024 + (np.arange(64) % 16)]).astype(f32)
        ang = (pos[:, None] * inv[None, :]).astype(f32)
        c = np.cos(ang).astype(f32).T
        s = np.sin(ang).astype(f32).T
        cos_full = np.concatenate([c, c], 0)
        sin_full = np.concatenate([-s, s], 0)
        sc = f32(128 ** -0.5)
        rope[seg, 0] = cos_full
        rope[seg, 1] = sin_full
        rope[seg, 2] = cos_full * sc
        rope[seg, 3] = sin_full * sc
    lam = np.log1p(-np.exp2(-5.0 - np.arange(8, dtype=f32))).astype(f32)
    i = np.arange(64, dtype=f32)
    diff = i[None, :] - i[:, None]
    dm = np.where(diff[:, None, :] >= 0, np.exp(lam[None, :, None] * np.maximum(diff[:, None, :], 0.0)), 0.0).astype(f32)
    g1 = np.exp(lam[:, None] * (i + 1.0)[None, :]).astype(f32)
    g1 = np.broadcast_to(g1.reshape(1, 512), (128, 512)).copy()
    te = np.zeros((64, 16), f32)
    te[:, 0:8] = np.exp(lam[None, :] * (63.0 - i)[:, None])
    te[0:16, 8:16] = np.exp(lam[None, :] * (15.0 - i[0:16])[:, None])
    jj = np.arange(576)[:, None]
    ii = np.arange(64)[None, :]
    idx = np.clip(512 + ii - jj, -256, 256) + 256
    biasT = np.ascontiguousarray(rel_bias[:, :, idx]).astype(f32)
    biasR = np.ascontiguousarray(biasT.reshape(2, 16, 9, 64, 64)[:, :, ::-1].reshape(2, 16, 576, 64))
    tri = (np.arange(64)[:, None] <= np.arange(64)[None, :]).astype(f32)
    pswap = np.zeros((128, 128), f32)
    for m in range(128):
        pswap[(m + 64) % 128, m] = 1.0
    return dict(c_rope=rope, c_biasT=biasT, c_biasR=biasR, c_dm=dm.reshape(64, 512), c_g1=g1, c_te=te,
                c_ident=np.eye(128, dtype=f32), c_tri=tri, c_pswap=pswap)


_NC_CACHE = {}


def kernel(x_prompt, x_sample, cache_attn_k, cache_attn_v, state_ret, state_ssm, state_conv,
           norm_mix, w_in, conv_w, conv_b, dt_bias, a_log, d_skip, ssm_norm, q_norm, k_norm,
           rel_bias, w_br_ssm, w_br_ret, w_br_att, w_out, norm_ffn, w_ffn_in, w_ffn_out,
           _n_layers=2, _n_segs=4, _cores=8):
    f = lambda a: np.ascontiguousarray(np.asarray(a, dtype=np.float32))
    key = (_n_layers, _n_segs)
    if key not in _NC_CACHE:
        _NC_CACHE[key] = build(_n_layers, _n_segs)
    nc = _NC_CACHE[key]
    consts = _consts(f(rel_bias))
    shared = dict(norm_mix=f(norm_mix), w_in=f(w_in), conv_w=f(conv_w), conv_b=f(conv_b), dt_bias=f(dt_bias), a_log=f(a_log),
                  d_skip=f(d_skip), ssm_norm=f(ssm_norm), q_norm=f(q_norm), k_norm=f(k_norm), w_br_ssm=f(w_br_ssm),
                  w_br_ret=f(w_br_ret), w_br_att=f(w_br_att), w_out=f(w_out), norm_ffn=f(norm_ffn), w_ffn_in=f(w_ffn_in),
                  w_ffn_out=f(w_ffn_out), **consts)
    x_prompt = f(x_prompt); x_sample = f(x_sample)
    ck = f(cache_attn_k); cvv = f(cache_attn_v); sr = f(state_ret); ss = f(state_ssm); sc = f(state_conv)
    in_maps = []
    for c in range(_cores):
        b = c % 4
        sl = slice(4 * c, 4 * c + 4)
        m = dict(shared)
        m.update(x_prompt=x_prompt[b], x_sample=x_sample[sl], cache_attn_k=np.ascontiguousarray(ck[:, sl]),
                 cache_attn_v=np.ascontiguousarray(cvv[:, sl]), state_ret=np.ascontiguousarray(sr[:, sl]),
                 state_ssm=np.ascontiguousarray(ss[:, sl]), state_conv=np.ascontiguousarray(sc[:, sl]))
        in_maps.append(m)
    res = run_bass_kernel_spmd(nc, in_maps, core_ids=list(range(_cores)))
    rs = res.results
    npc = min(4, _cores)
    yp = np.stack([rs[b]["y_p"] for b in range(npc)])
    ys = np.concatenate([rs[c]["y_s"].reshape(4, 16, D) for c in range(_cores)])
    pk = np.stack([rs[b]["k_p"] for b in range(npc)], 1)
    pv = np.stack([rs[b]["v_p"] for b in range(npc)], 1)
    pr = np.stack([rs[b]["ret_p"] for b in range(npc)], 1)
    pssm = np.stack([rs[b]["ssm_p"].reshape(2, 32, 64, 128) for b in range(npc)], 1)
    pconv = np.stack([rs[b]["conv_p"] for b in range(npc)], 1)
    sk = np.concatenate([rs[c]["k_s"].reshape(2, 4, 16, 16, 128) for c in range(_cores)], 1)
    sv = np.concatenate([rs[c]["v_s"].reshape(2, 4, 16, 16, 128) for c in range(_cores)], 1)
    srr = np.concatenate([rs[c]["ret_s"] for c in range(_cores)], 1)
    sssm = np.concatenate([rs[c]["ssm_s"].reshape(2, 4, 32, 64, 128) for c in range(_cores)], 1)
    sconv = np.concatenate([rs[c]["conv_s"] for c in range(_cores)], 1)
    return (yp, ys, pk, pv, pr, pssm, pconv, sk, sv, srr, sssm, sconv)
```

```python
import math
import os
import numpy as np
import concourse.bass as bass
import concourse.mybir as mybir
from concourse.bass_utils import run_bass_kernel_spmd

F32 = mybir.dt.float32
BF16 = mybir.dt.bfloat16
AF = mybir.ActivationFunctionType
ALU = mybir.AluOpType

D = 2048
SEG = 512
NSEG_FULL = 4
EPS = 1e-6
Z0, XBC0, DT0, RQ0, RK0, RV0, RG0, AQ0, AK0, AV0, GA0, GB0, GC0 = (
    0, 2048, 5120, 5152, 6176, 7200, 9248, 11296, 13344, 15392, 17440, 19488, 21536)
NIN = 23584
FFH = 5632
NSLAB = 3
LAM = [math.log1p(-2.0 ** (-5.0 - h)) for h in range(8)]
ATT_SCALE = 128 ** -0.5


class R:
    def __init__(self, name="", excl=False):
        self.name = name
        self.w = None
        self.rs = {}
        self.excl = excl


def _split(reads, writes):
    ex = [r for r in reads if r.excl]
    if not ex:
        return list(reads), list(writes)
    return [r for r in reads if not r.excl], list(writes) + [r for r in ex if r not in writes]


class Sched:
    def __init__(self, nc):
        self.nc = nc
        self.engs = {}
        for name, e in (("pe", nc.tensor), ("act", nc.scalar), ("dve", nc.vector), ("pool", nc.gpsimd), ("sp", nc.sync)):
            self.engs[name] = dict(name=name, e=e, sem=nc.alloc_semaphore(name="s_" + name), cnt=0, seen={})
        self.queues = {}
        for qn, en, nq in (("qsp", "sp", 4), ("qpool", "pool", 4)):
            self.queues[qn] = dict(eng=en, sems=[nc.alloc_semaphore(name=f"{qn}{i}") for i in range(nq)], k=0, cnts=[0] * nq)

    def _wait(self, E, toks):
        best = {}
        for key, sem, v in toks:
            if v <= E["seen"].get(key, 0):
                continue
            if key not in best or best[key][1] < v:
                best[key] = (sem, v)
        for key, (sem, v) in best.items():
            E["e"].wait_ge(sem, v)
            E["seen"][key] = v

    def _deps(self, E, reads, writes, same_ok=False):
        toks = []
        for r in reads:
            if r.w is not None:
                toks.append(r.w)
        for r in writes:
            if r.w is not None:
                toks.append(r.w)
            toks.extend(r.rs.values())
        if same_ok:
            toks = [t for t in toks if t[0] != E["name"]]
        return toks

    def _mark(self, key, tok, reads, writes):
        for r in reads:
            r.rs[key] = tok
        for r in writes:
            r.w = tok
            r.rs = {}

    def op(self, en, fn, reads=(), writes=()):
        self.group(en, [fn], reads, writes)

    def group(self, en, fns, reads=(), writes=()):
        reads, writes = _split(reads, writes)
        E = self.engs[en]
        self._wait(E, self._deps(E, reads, writes, same_ok=(en == "pe")))
        ins = None
        for fn in fns:
            ins = fn()
        E["cnt"] += 1
        ins.then_inc(E["sem"], 1)
        self._mark(E["name"], (E["name"], E["sem"], E["cnt"]), reads, writes)

    def dma(self, qn, out, in_, reads=(), writes=(), **kw):
        Q = self.queues[qn]
        E = self.engs[Q["eng"]]
        nq = len(Q["sems"])
        i = Q["k"] % nq
        toks = self._deps(E, reads, writes)
        key = f"{qn}{i}"
        if Q["cnts"][i] > 0:
            toks.append((key, Q["sems"][i], Q["cnts"][i]))
        self._wait(E, toks)
        ins = E["e"].dma_start(out=out, in_=in_, **kw)
        Q["cnts"][i] += 16
        Q["k"] += 1
        ins.then_inc(Q["sems"][i], 16)
        self._mark(key, (key, Q["sems"][i], Q["cnts"][i]), reads, writes)

    def _all_toks(self):
        toks = []
        for e in self.engs.values():
            if e["cnt"] > 0:
                toks.append((e["name"], e["sem"], e["cnt"]))
        for qn, Q in self.queues.items():
            for i, s in enumerate(Q["sems"]):
                if Q["cnts"][i] > 0:
                    toks.append((f"{qn}{i}", s, Q["cnts"][i]))
        return toks

    def barrier(self, engines=("pe", "act", "dve", "sp", "pool")):
        toks = self._all_toks()
        for en in engines:
            self._wait(self.engs[en], toks)

    def finish(self):
        self._wait(self.engs["sp"], self._all_toks())


def build(n_layers=2, n_segs=4):
    nc = bass.Bass("TRN2", target_bir_lowering=False)
    S = Sched(nc)

    def din(name, shape):
        return nc.dram_tensor(name, list(shape), F32, kind="ExternalInput").ap()

    def dout(name, shape):
        return nc.dram_tensor(name, list(shape), F32, kind="ExternalOutput").ap()

    def dscr(name, shape, dt):
        return nc.dram_tensor(name, list(shape), dt, kind="Internal").ap()

    def sb(name, shape, dt=F32):
        return nc.alloc_sbuf_tensor(name, list(shape), dt).ap()

    xp_d = din("x_prompt", [2048, D])
    xs_d = din("x_sample", [4, 16, D])
    ck_d = din("cache_attn_k", [2, 4, 512, 16, 128])
    cv_d = din("cache_attn_v", [2, 4, 512, 16, 128])
    sret_d = din("state_ret", [2, 4, 8, 128, 256])
    sssm_d = din("state_ssm", [2, 4, 32, 64, 128])
    sconv_d = din("state_conv", [2, 4, 3, 3072])
    norm_mix_d = din("norm_mix", [2, D])
    w_in_d = din("w_in", [2, D, NIN])
    conv_w_d = din("conv_w", [2, 4, 3072])
    conv_b_d = din("conv_b", [2, 3072])
    dt_bias_d = din("dt_bias", [2, 32])
    a_log_d = din("a_log", [2, 32])
    d_skip_d = din("d_skip", [2, 32])
    ssm_norm_d = din("ssm_norm", [2, D])
    q_norm_d = din("q_norm", [2, 128])
    k_norm_d = din("k_norm", [2, 128])
    w_br_ssm_d = din("w_br_ssm", [2, D, D])
    w_br_ret_d = din("w_br_ret", [2, D, D])
    w_br_att_d = din("w_br_att", [2, D, D])
    w_out_d = din("w_out", [2, D, D])
    norm_ffn_d = din("norm_ffn", [2, D])
    w_ffn_in_d = din("w_ffn_in", [2, D, 2 * FFH])
    w_ffn_out_d = din("w_ffn_out", [2, FFH, D])
    rope_d = din("c_rope", [4, 4, 128, 576])
    bias_d = din("c_biasT", [2, 16, 576, 64])
    biasr_d = din("c_biasR", [2, 16, 576, 64])
    dm_d = din("c_dm", [64, 512])
    g1_d = din("c_g1", [128, 512])
    te_d = din("c_te", [64, 16])
    ident_d = din("c_ident", [128, 128])
    tri_d = din("c_tri", [64, 64])
    pswap_d = din("c_pswap", [128, 128])

    y_p = dout("y_p", [2048, D])
    y_s = dout("y_s", [64, D])
    k_p = dout("k_p", [2, 512, 16, 128])
    v_p = dout("v_p", [2, 512, 16, 128])
    ret_p = dout("ret_p", [2, 8, 128, 256])
    ssm_p = dout("ssm_p", [2, 2048, 128])
    conv_p = dout("conv_p", [2, 3, 3072])
    k_s = dout("k_s", [2, 64, 16, 128])
    v_s = dout("v_s", [2, 64, 16, 128])
    ret_s = dout("ret_s", [2, 4, 8, 128, 256])
    ssm_s = dout("ssm_s", [2, 4, 2048, 128])
    conv_s = dout("conv_s", [2, 4, 3, 3072])

    hscr = dscr("hscr", [2, 128, 2048], F32); r_hscr = [R(), R()]
    sscr = dscr("sscr", [2, 128, 2048], F32); r_sscr = [R(), R()]
    khd = dscr("khd", [2, 2, 128, 16, 512], BF16)
    vhd = dscr("vhd", [2, 2, 128, 16, 512], BF16)
    r_khd = [[[R() for _ in range(4)] for _ in range(2)] for _ in range(2)]
    r_vhd = [[[R() for _ in range(4)] for _ in range(2)] for _ in range(2)]

    NCOLMAX = 576
    xT = sb("xT", [128, 16, NCOLMAX]); r_x = [R() for _ in range(16)]
    hnT = sb("hnT", [128, 16, NCOLMAX], BF16); r_hn = [R() for _ in range(16)]
    br = sb("br", [128, 16, NCOLMAX], BF16); r_br = [R() for _ in range(16)]
    mT = sb("mT", [128, 16, NCOLMAX], BF16); r_m = [R() for _ in range(16)]
    slabs = [sb(f"slab{i}", [128, 8192], BF16) for i in range(NSLAB)]
    r_slab = [R() for _ in range(NSLAB)]
    ident = sb("ident", [128, 128]); identb = sb("identb", [128, 128], BF16)
    meanD = sb("meanD", [128, 128], BF16); mean512 = sb("mean512", [128, 128], BF16)
    mean256 = sb("mean256", [128, 128], BF16); mean128 = sb("mean128", [128, 128], BF16)
    pswapf = sb("pswapf", [128, 128]); pswapb = sb("pswapb", [128, 128], BF16)
    ones64b = sb("ones64b", [64, 128], BF16); onesf = sb("onesf", [64, 128])
    tri = sb("tri", [64, 64]); dm = sb("dm", [64, 512]); g1 = sb("g1", [128, 512]); te = sb("te", [64, 16])
    r_c = R("consts")
    gm = sb("gm", [128, 16]); gf = sb("gf", [128, 16]); cw = sb("cw", [128, 24, 4]); cb = sb("cb", [128, 24])
    nw = sb("nw", [128, 16]); dcol = sb("dcol", [128, 16]); qn = sb("qn", [128, 1]); kn = sb("kn", [128, 1])
    dtb = sb("dtb", [64, 32]); abc = sb("abc", [64, 32])
    convc = sb("convc", [128, 2, 24, 3]); r_convc = [R(), R()]
    r_par = R("params")
    UN = None
    U_holder = []

    class Carver:
        def __init__(self):
            self.o = 0

        def F(self, n, parts=128):
            U = U_holder[0]
            a = U[0:parts, self.o:self.o + n]
            self.o += (n + 7) // 8 * 8
            assert self.o <= U_holder[1], (self.o, U_holder[1])
            return a

        def B(self, n, parts=128):
            U = U_holder[0]
            w = (n + 1) // 2
            a = U[0:parts, self.o:self.o + w].bitcast(BF16)[:, 0:n]
            self.o += (w + 7) // 8 * 8
            assert self.o <= U_holder[1], (self.o, U_holder[1])
            return a

    pj = [nc.alloc_psum_tensor(f"pj{i}", [128, 512], F32).ap() for i in range(2)]
    r_pj = [R(excl=True) for _ in range(2)]
    pjs_t = nc.alloc_psum_tensor("pjs", [128, 512], F32).ap()
    _rpjs = R(excl=True)
    r_pjs = [_rpjs for _ in range(8)]
    aux = [nc.alloc_psum_tensor(f"aux{i}", [128, 512], F32).ap() for i in range(4)]
    r_aux = [R(excl=True) for _ in range(4)]
    trb_t = nc.alloc_psum_tensor("trb", [128, 1024], BF16).ap()
    _rtrb = R(excl=True)
    r_trb = [_rtrb, _rtrb]
    rot = dict(pj=0, pjs=0, aux=0, trb=0)

    def next_pj():
        i = rot["pj"] % 2; rot["pj"] += 1
        return pj[i], r_pj[i]

    def next_pjs():
        i = rot["pjs"] % 8; rot["pjs"] += 1
        return pjs_t[:, i * 64:(i + 1) * 64], r_pjs[i]

    def next_aux():
        i = rot["aux"] % 4; rot["aux"] += 1
        return aux[i], r_aux[i]

    def next_trb():
        i = rot["trb"] % 2; rot["trb"] += 1
        return trb_t[:, i * 512:(i + 1) * 512], r_trb[i]

    def ps_for(tn):
        return next_pj() if tn > 64 else next_pjs()

    UN = (nc.sbuf_bytes_remaining - 512) // 4
    U_holder.append(sb("U", [128, UN]))
    U_holder.append(UN)
    V = nc.vector
    A = nc.scalar
    PE = nc.tensor

    def vop(fn, reads, writes):
        S.op("dve", fn, reads, writes)

    def aop(fn, reads, writes):
        S.op("act", fn, reads, writes)

    def act(out, in_, func, reads, writes, **kw):
        S.op("act", lambda: A.activation(out=out, in_=in_, func=func, **kw), reads, writes)

    def vtt(out, in0, in1, op, reads, writes):
        S.op("dve", lambda: V.tensor_tensor(out=out, in0=in0, in1=in1, op=op), reads, writes)

    def vcopy(out, in_, reads, writes):
        S.op("dve", lambda: V.tensor_copy(out=out, in_=in_), reads, writes)

    def acopy(out, in_, reads, writes):
        S.op("act", lambda: A.copy(out=out, in_=in_), reads, writes)

    def mm(out, lhsT, rhs, start=True, stop=True, skip=False):
        if skip:
            return lambda: PE.matmul(out, lhsT=lhsT, rhs=rhs, start=start, stop=stop, skip_group_check=True)
        return lambda: PE.matmul(out, lhsT=lhsT, rhs=rhs, start=start, stop=stop)

    def tr(out, in_, idn):
        return lambda: PE.transpose(out, in_, idn)

    S.dma("qsp", ident[:], ident_d[:, :], writes=[r_c])
    S.dma("qsp", tri[:], tri_d[:, :], writes=[r_c])
    S.dma("qsp", pswapf[:], pswap_d[:, :], writes=[r_c])
    S.dma("qsp", dm[:], dm_d[:, :], writes=[r_c])
    S.dma("qsp", g1[:], g1_d[:, :], writes=[r_c])
    S.dma("qsp", te[:], te_d[:, :], writes=[r_c])
    vcopy(identb[:], ident[:], [r_c], [r_c])
    vcopy(pswapb[:], pswapf[:], [r_c], [r_c])
    vop(lambda: V.memset(meanD[:], 1.0 / 2048), [], [r_c])
    vop(lambda: V.memset(mean512[:], 1.0 / 512), [], [r_c])
    vop(lambda: V.memset(mean256[:], 1.0 / 256), [], [r_c])
    vop(lambda: V.memset(mean128[:], 1.0 / 128), [], [r_c])
    vop(lambda: V.memset(ones64b[:], 1.0), [], [r_c])
    vop(lambda: V.memset(onesf[:], 1.0), [], [r_c])

    def colvec(dst, src_vec, nchunk):
        S.dma("qsp", dst, src_vec.rearrange("(c p) -> p c", p=128), writes=[r_par], allow_slow_non_contiguous=True)

    def load_params(l):
        S.barrier()
        colvec(gm[:], norm_mix_d[l], 16)
        colvec(gf[:], norm_ffn_d[l], 16)
        colvec(nw[:], ssm_norm_d[l], 16)
        colvec(cb[:], conv_b_d[l], 24)
        for k in range(4):
            S.dma("qsp", cw[:, :, k], conv_w_d[l, k].rearrange("(c p) -> p c", p=128), writes=[r_par], allow_slow_non_contiguous=True)
        S.dma("qsp", qn[:], q_norm_d[l].rearrange("(p o) -> p o", o=1), writes=[r_par], allow_slow_non_contiguous=True)
        S.dma("qsp", kn[:], k_norm_d[l].rearrange("(p o) -> p o", o=1), writes=[r_par], allow_slow_non_contiguous=True)
        dsv = d_skip_d[l].rearrange("(c two) -> two c", two=2)
        for hh in range(2):
            S.dma("qsp", dcol[64 * hh:64 * hh + 64, :], dsv[hh:hh + 1, :].partition_broadcast(64), writes=[r_par],
                  allow_slow_non_contiguous=True)
        S.dma("qsp", dtb[:], dt_bias_d[l:l + 1, :].partition_broadcast(64), writes=[r_par])
        S.dma("qsp", abc[:], a_log_d[l:l + 1, :].partition_broadcast(64), writes=[r_par])
        act(abc[:], abc[:], AF.Exp, [r_par], [r_par])
        vop(lambda: V.tensor_scalar(out=abc[:], in0=abc[:], scalar1=-1.0, scalar2=None, op0=ALU.mult), [r_par], [r_par])

    def run_jobs(jobs):
        loads = [i for i, j in enumerate(jobs) if j[0] is not None]
        slot_of = {ji: n % NSLAB for n, ji in enumerate(loads)}
        st = dict(nxt=0)

        def issue_next():
            if st["nxt"] < len(loads):
                ji = loads[st["nxt"]]
                jobs[ji][0](slot_of[ji])
                st["nxt"] += 1

        for _ in range(NSLAB):
            issue_next()
        for ji, (ld, fn) in enumerate(jobs):
            if ld is not None:
                fn(slot_of[ji])
                issue_next()
            else:
                fn(None)

    def wview(w2d):
        return w2d.rearrange("(kc p) c -> p kc c", p=128)

    def proj_job(jobs, wv, c0, ncols, rhs_buf, rhs_regs, tiles, consumer):
        def ld(slot):
            sv = slabs[slot][:, 0:16 * ncols].rearrange("p (k c) -> p k c", k=16)
            S.dma("qpool", sv, wv[:, :, c0:c0 + ncols], writes=[r_slab[slot]])

        def fn(slot):
            sv = slabs[slot][:, 0:16 * ncols].rearrange("p (k c) -> p k c", k=16)
            for cc in range(ncols // 128):
                outs = []
                for (t0, tn) in tiles:
                    ps, rps = ps_for(tn)
                    S.group("pe", [mm(ps[:, 0:tn], sv[:, k, cc * 128:(cc + 1) * 128], rhs_buf[:, k, t0:t0 + tn],
                                      k == 0, k == 15) for k in range(16)],
                            [r_slab[slot]] + rhs_regs, [rps])
                    outs.append((t0, tn, ps, rps))
                consumer(cc, outs)
        jobs.append((ld, fn))

    def layer_pass(l, seg, last_layer):
        has_s = (seg == 0)
        last_seg = (seg == n_segs - 1)
        ncol = 576 if has_s else 512
        tiles = [(0, 512)] + ([(512, 64)] if has_s else [])
        pchunks = [(64 * c, 64) for c in range(8)]
        schunks = [(512 + 16 * i, 16) for i in range(4)] if has_s else []
        allq = pchunks + schunks
        w_in_v = wview(w_in_d[l])
        jobs = []

        def J(fn):
            jobs.append((None, lambda _s, fn=fn: fn()))

        def sumsq_rstd(srcs, meanm, rstd, tmpsq, r_tmpsq, reads_of):
            pss = [(t0, tn) + ps_for(tn) for (t0, tn) in tiles]
            n = len(srcs)
            for i, (src, rr) in enumerate(zip(srcs, reads_of)):
                sq = tmpsq[i % 2]
                act(sq[:, 0:ncol], src, AF.Square, [rr], [r_tmpsq[i % 2]])
                for (t0, tn, ps, rps) in pss:
                    S.group("pe", [mm(ps[:, 0:tn], meanm[:], sq[:, t0:t0 + tn], i == 0, i == n - 1)], [r_tmpsq[i % 2], r_c], [rps])
            for (t0, tn, ps, rps) in pss:
                act(rstd[:, t0:t0 + tn], ps[:, 0:tn], AF.Ln, [rps], [r_rstd], bias=EPS, scale=1.0)
            act(rstd[:, 0:ncol], rstd[:, 0:ncol], AF.Exp, [r_rstd], [r_rstd], scale=-0.5)

        r_rstd = R()

        def phase0():
            load_params(l)
            cv = Carver()
            xin = cv.F(2048)
            r_xin = R()
            if l == 0:
                srcs = [(xp_d[seg * 512 + 128 * tt: seg * 512 + 128 * tt + 128, :], 128, 128 * tt) for tt in range(4)]
                if has_s:
                    srcs.append((xs_d.rearrange("b t d -> (b t) d"), 64, 512))
                for (src, nt, c0) in srcs:
                    S.dma("qsp", xin[0:nt, :], src, writes=[r_xin])
                    for q in range(4):
                        ps, rps = next_aux()
                        S.group("pe", [tr(ps[:, j * nt:(j + 1) * nt], xin[0:nt, (4 * q + j) * 128:(4 * q + j + 1) * 128], ident[0:nt, 0:nt])
                                       for j in range(4)], [r_xin, r_c], [rps])
                        vcopy(xT[:, 4 * q:4 * q + 4, c0:c0 + nt], ps[:, 0:4 * nt].rearrange("p (j t) -> p j t", j=4),
                              [rps], r_x[4 * q:4 * q + 4])
            rstd = cv.F(576)
            tsq = [cv.B(576), cv.B(576)]
            rts = [R(), R()]
            sumsq_rstd([xT[:, c, 0:ncol] for c in range(16)], meanD, rstd, tsq, rts, r_x)
            for c in range(16):
                vop(lambda c=c: V.scalar_tensor_tensor(out=hnT[:, c, 0:ncol], in0=xT[:, c, 0:ncol], scalar=gm[:, c:c + 1],
                                                       in1=rstd[:, 0:ncol], op0=ALU.mult, op1=ALU.mult),
                    [r_x[c], r_par, r_rstd], [r_hn[c]])
            S.barrier()
        J(phase0)
        marks = {"p0": len(jobs)}

        def branch_jobs(cv, w_br, gate0, first):
            cv.o = max(cv.o, int(os.environ.get("KGOFF", 0)))
            G = [cv.F(4 * 576), cv.F(4 * 576)]
            rG = [R(), R()]
            tmp = [cv.F(576), cv.F(576)]
            rtmp = [R(), R()]
            wv = wview(w_br[l])
            for j in range(4):
                Gj = G[j % 2].rearrange("p (c t) -> p c t", c=4)

                def gcons(cc, outs, Gj=Gj, j=j):
                    for (t0, tn, ps, rps) in outs:
                        if os.environ.get("KNOG"):
                            continue
                        act(Gj[:, cc, t0:t0 + tn], ps[:, 0:tn], AF.Sigmoid, [rps], [rG[j % 2]])
                kbr = os.environ.get("KBR", "gb")
                if "g" in kbr:
                    g0x = int(os.environ.get("KGATE0", gate0))
                    proj_job(jobs, w_in_v, g0x + 512 * j, 512, hnT, r_hn, tiles, gcons)

                def bcons(cc, outs, Gj=Gj, j=j):
                    c = 4 * j + cc
                    for (t0, tn, ps, rps) in outs:
                        if first:
                            vtt(mT[:, c, t0:t0 + tn], ps[:, 0:tn], Gj[:, cc, t0:t0 + tn], ALU.mult, [rps, rG[j % 2]], [r_m[c]])
                        else:
                            i = rot["aux"] % 2
                            vtt(tmp[i][:, 0:tn], ps[:, 0:tn], Gj[:, cc, t0:t0 + tn], ALU.mult, [rps, rG[j % 2]], [rtmp[i]])
                            vtt(mT[:, c, t0:t0 + tn], mT[:, c, t0:t0 + tn], tmp[i][:, 0:tn], ALU.add, [rtmp[i], r_m[c]], [r_m[c]])
                            rot["aux"] += 1
                if "b" in kbr:
                    proj_job(jobs, wv, 512 * j, 512, br, r_br, tiles, bcons)

        cv1 = Carver()
        xpb = [cv1.F(591), cv1.F(591)]; r_xp = [R(), R()]
        cacc = [cv1.F(576), cv1.F(576)]; r_cacc = [R(), R()]
        hs = cv1.F(288).rearrange("p (c b r) -> p c b r", c=24, b=4); r_hs = R()
        off_hso = cv1.o
        hso = cv1.F(288).rearrange("p (c b r) -> p c b r", c=24, b=4); r_hso = R()
        dtt = cv1.F(384, 64).rearrange("p (q h) -> p q h", q=12)
        dtA = cv1.F(384, 64).rearrange("p (q h) -> p q h", q=12)
        r_dt = R()
        acum = cv1.F(32, 64); wsd = cv1.F(32, 64); wdt = cv1.F(32, 64); etot = cv1.F(32); r_ch = R()
        Rb = cv1.F(512, 64); r_Rb = R()
        Db = cv1.F(512, 64); r_Db = R()
        Eb = Db; r_Eb = r_Db
        eA = cv1.F(512); r_eA = R()
        CBm = cv1.F(64, 64); r_CBm = R()
        hT = cv1.F(2048); r_hT = [R() for _ in range(4)]
        stg = cv1.F(2048); r_stg = R()
        BT = cv1.B(4 * 576).rearrange("p (g t) -> p g t", g=4); r_BT = [R() for _ in range(4)]
        CT = cv1.B(4 * 576).rearrange("p (g t) -> p g t", g=4); r_CT = [R() for _ in range(4)]
        dtw = cv1.B(512).rearrange("p (k c) -> p k c", k=16); r_dtw = R()
        MT = cv1.B(512, 64); r_MT = R()
        xt = cv1.B(512, 64); r_xt = R()
        xd = cv1.B(512, 64); r_xd = R()
        xs2 = cv1.B(512, 64); r_xs2 = R()
        Cp = cv1.B(512); r_Cp = R()
        ytb = cv1.B(512, 64); r_ytb = R()
        hTb = cv1.B(2048); r_hTb = [R() for _ in range(4)]
        Btok = cv1.B(128, 64); r_Btok = R()
        cv1b = Carver()
        rstd1 = cv1b.F(576)
        tsq1 = [cv1b.B(576), cv1b.B(576)]; rts1 = [R(), R()]
        cvc = Carver()
        Rb_b = cvc.F(512, 64); Db_b = cvc.F(512, 64); eA_b = cvc.F(512); CBm_b = cvc.F(64, 64)
        MT_b = cvc.B(512, 64); xt_b = cvc.B(512, 64); xd_b = cvc.B(512, 64); xs2_b = cvc.B(512, 64)
        assert cvc.o <= off_hso, (cvc.o, off_hso)
        Cp_b = cv1.B(512); ytb_b = cv1.B(512, 64); Btok_b = cv1.B(128, 64)
        rD_b = R()
        ssd_sets = [
            (Rb, r_Rb, Db, r_Db, Eb, r_Eb, eA, r_eA, CBm, r_CBm, MT, r_MT, xt, r_xt, xd, r_xd, xs2, r_xs2, Cp, r_Cp, ytb, r_ytb,
             Btok, r_Btok),
            (Rb_b, R(), Db_b, rD_b, Db_b, rD_b, eA_b, R(), CBm_b, R(), MT_b, R(), xt_b, R(), xd_b, R(), xs2_b, R(), Cp_b, R(),
             ytb_b, R(), Btok_b, R()),
        ]
        szt = tsq1; r_szt = rts1

        def ssd_pre():
            S.dma("qpool", dtw, w_in_v[:, :, DT0:DT0 + 32], writes=[r_dtw])
            if has_s:
                for b in range(4):
                    for r in range(3):
                        S.dma("qsp", hs[:, :, b, r], sconv_d[l, b, r].rearrange("(c p) -> p c", p=128), writes=[r_hs],
                              allow_slow_non_contiguous=True)
        J(ssd_pre)

        for j in range(6):
            def xcons(cc, outs, j=j):
                ci = 4 * j + cc
                i = ci % 2
                xp = xpb[i]
                xps = xp[:, 515:591].rearrange("p (b t) -> p b t", b=4)
                for (t0, tn, ps, rps) in outs:
                    if t0 == 0:
                        acopy(xp[:, 3:515], ps[:, 0:512], [rps], [r_xp[i]])
                    else:
                        acopy(xps[:, :, 3:19], ps[:, 0:64].rearrange("p (b t) -> p b t", b=4), [rps], [r_xp[i]])
                if seg == 0:
                    vop(lambda: V.memset(xp[:, 0:3], 0.0), [], [r_xp[i]])
                else:
                    vcopy(xp[:, 0:3], convc[:, l, ci, :], [r_convc[l]], [r_xp[i]])
                vcopy(convc[:, l, ci, :], xp[:, 512:515], [r_xp[i]], [r_convc[l]])
                if has_s:
                    vcopy(xps[:, :, 0:3], hs[:, ci, :, :], [r_hs], [r_xp[i]])
                    vcopy(hso[:, ci, :, :], xps[:, :, 16:19], [r_xp[i]], [r_hso])
                acc = cacc[i]
                accs = acc[:, 512:576].rearrange("p (b t) -> p b t", b=4)
                vop(lambda: V.tensor_scalar(out=acc[:, 0:512], in0=xp[:, 0:512], scalar1=cw[:, ci, 0:1], scalar2=None, op0=ALU.mult),
                    [r_xp[i], r_par], [r_cacc[i]])
                for k in range(1, 4):
                    vop(lambda k=k: V.scalar_tensor_tensor(out=acc[:, 0:512], in0=xp[:, k:k + 512], scalar=cw[:, ci, k:k + 1],
                                                           in1=acc[:, 0:512], op0=ALU.mult, op1=ALU.add),
                        [r_xp[i], r_par], [r_cacc[i]])
                if has_s:
                    vop(lambda: V.tensor_scalar(out=accs, in0=xps[:, :, 0:16], scalar1=cw[:, ci, 0:1], scalar2=None, op0=ALU.mult),
                        [r_xp[i], r_par], [r_cacc[i]])
                    for k in range(1, 4):
                        vop(lambda k=k: V.scalar_tensor_tensor(out=accs, in0=xps[:, :, k:k + 16], scalar=cw[:, ci, k:k + 1],
                                                               in1=accs, op0=ALU.mult, op1=ALU.add),
                            [r_xp[i], r_par], [r_cacc[i]])
                if ci < 16:
                    dst, rd = br[:, ci, 0:ncol], r_br[ci]
                elif ci < 20:
                    dst, rd = BT[:, ci - 16, 0:ncol], r_BT[ci - 16]
                else:
                    dst, rd = CT[:, ci - 20, 0:ncol], r_CT[ci - 20]
                act(dst, acc[:, 0:ncol], AF.Silu, [r_cacc[i], r_par], [rd], bias=cb[:, ci:ci + 1], scale=1.0)
            proj_job(jobs, w_in_v, XBC0 + 512 * j, 512, hnT, r_hn, tiles, xcons)

        def ssd_dt():
            nq = len(allq)
            ps, rps = next_aux()
            for q, (c0, L) in enumerate(allq):
                S.group("pe", [mm(ps[0:L, 32 * q:32 * q + 32], hnT[:, k, c0:c0 + L], dtw[:, k, :], k == 0, k == 15) for k in range(16)],
                        r_hn + [r_dtw], [rps])
            psv = ps[0:64, 0:32 * nq].rearrange("p (q h) -> p q h", q=nq)
            vtt(dtt[:, 0:nq, :], psv, dtb[:].unsqueeze(1).to_broadcast([64, nq, 32]), ALU.add, [rps, r_par], [r_dt])
            act(dtt[:, 0:nq, :], dtt[:, 0:nq, :], AF.Exp, [r_dt], [r_dt])
            act(dtt[:, 0:nq, :], dtt[:, 0:nq, :], AF.Ln, [r_dt], [r_dt], bias=1.0, scale=1.0)
            vtt(dtA[:, 0:nq, :], dtt[:, 0:nq, :], abc[:].unsqueeze(1).to_broadcast([64, nq, 32]), ALU.mult, [r_dt, r_par], [r_dt])
        marks["conv"] = len(jobs)
        J(ssd_dt)
        J(S.barrier)
        marks["dt"] = len(jobs)

        chs = [dict(acum=acum, wsd=wsd, wdt=wdt, etot=etot, r=r_ch)]
        chs.append(dict(acum=cv1.F(32, 64), wsd=cv1.F(32, 64), wdt=cv1.F(32, 64), etot=cv1.F(32), r=R()))
        slot_aux = [dict(i=0), dict(i=0), dict(i=0)]
        _cbanks = [(aux[0], r_aux[0]), (aux[1], r_aux[1]), (aux[2], r_aux[2]), (aux[3], r_aux[3]), (pj[0], r_pj[0]), (pj[1], r_pj[1])]

        def chain_aux(slot):
            k = 2 * slot + (slot_aux[slot]["i"] % 2)
            slot_aux[slot]["i"] += 1
            return _cbanks[k]

        def ssd_prologue(q, c0, L, slot):
            ch = chs[q % 2]
            acum, wsd, wdt, etot, r_ch = ch["acum"], ch["wsd"], ch["wdt"], ch["etot"], ch["r"]
            ps1, rps1 = chain_aux(slot)
            S.group("pe", [mm(ps1[0:L, 0:32], tri[0:L, 0:L], dtA[0:L, q, :]),
                           mm(ps1[:, 32:64], onesf[0:L, :], dtA[0:L, q, :])], [r_c, r_dt], [rps1])
            vcopy(acum[0:L, :], ps1[0:L, 0:32], [rps1], [r_ch])
            act(etot[:, :], ps1[:, 32:64], AF.Exp, [rps1], [r_ch])
            vtt(wsd[0:L, :], ps1[0:L, 32:64], acum[0:L, :], ALU.subtract, [rps1, r_ch], [r_ch])
            act(wsd[0:L, :], wsd[0:L, :], AF.Exp, [r_ch], [r_ch])
            vtt(wdt[0:L, :], wsd[0:L, :], dtt[0:L, q, :], ALU.mult, [r_ch, r_dt], [r_ch])

        def ssd_gchain(q, c0, L, has_state, g, slot):
            cols = slice(c0, c0 + L)
            if g == 0:
                ssd_prologue(q, c0, L, slot)
            ch = chs[q % 2]
            acum, wdt, etot, r_ch = ch["acum"], ch["wdt"], ch["etot"], ch["r"]
            (Rb, r_Rb, Db, r_Db, Eb, r_Eb, eA, r_eA, CBm, r_CBm, MT, r_MT, xt, r_xt, xd, r_xd, xs2, r_xs2, Cp, r_Cp, ytb, r_ytb,
             Btok, r_Btok) = ssd_sets[slot]

            def v3(buf, parts=None):
                p = L if parts is None else parts
                return buf[0:p, 0:8 * L].rearrange("p (h i) -> p h i", h=8)
            x3 = lambda b: b[0:L, :].rearrange("p (h d) -> p h d", h=8)
            vtt(v3(Rb), tri[0:L, 0:L].unsqueeze(1).to_broadcast([L, 8, L]),
                dtA[0:L, q, 8 * g:8 * g + 8].unsqueeze(2).to_broadcast([L, 8, L]), ALU.mult, [r_c, r_dt], [r_Rb])
            yield
            psA, rpsA = chain_aux(slot)
            S.group("pe", [mm(psA[:, 0:8 * L], onesf[0:L, :], Rb[0:L, 0:8 * L])], [r_c, r_Rb], [rpsA])
            pscb, rpscb = chain_aux(slot)
            S.group("pe", [mm(pscb[0:L, 0:L], BT[:, g, cols], CT[:, g, cols])], [r_BT[g], r_CT[g]], [rpscb])
            yield
            vtt(CBm[0:L, 0:L], pscb[0:L, 0:L], tri[0:L, 0:L], ALU.mult, [rpscb, r_c], [r_CBm])
            vtt(v3(Db), v3(psA), acum[0:L, 8 * g:8 * g + 8].unsqueeze(2).to_broadcast([L, 8, L]), ALU.subtract,
                [rpsA, r_ch], [r_Db])
            yield
            vop(lambda: V.tensor_scalar(out=Db[0:L, 0:8 * L], in0=Db[0:L, 0:8 * L], scalar1=0.0, scalar2=None, op0=ALU.min),
                [r_Db], [r_Db])
            if has_state:
                act(eA[:, 0:8 * L], psA[:, 0:8 * L], AF.Exp, [rpsA], [r_eA])
            yield
            act(Eb[0:L, 0:8 * L], Db[0:L, 0:8 * L], AF.Exp, [r_Db], [r_Eb])
            if has_state:
                vtt(v3(Cp, 128), v3(eA, 128), CT[:, g, cols].unsqueeze(1).to_broadcast([128, 8, L]), ALU.mult,
                    [r_eA, r_CT[g]], [r_Cp])
            yield
            vtt(v3(MT), v3(Eb), CBm[0:L, 0:L].unsqueeze(1).to_broadcast([L, 8, L]), ALU.mult, [r_Eb, r_CBm], [r_MT])
            pst, rpst = next_trb()
            S.group("pe", [tr(pst[0:L, cc * 128:(cc + 1) * 128], br[:, 4 * g + cc, cols], identb[:]) for cc in range(4)],
                    r_br[4 * g:4 * g + 4] + [r_c], [rpst])
            acopy(xt[0:L, :], pst[0:L, :], [rpst], [r_xt])
            yield
            vtt(x3(xd), x3(xt), dtt[0:L, q, 8 * g:8 * g + 8].unsqueeze(2).to_broadcast([L, 8, 64]), ALU.mult, [r_xt, r_dt], [r_xd])
            vtt(x3(xs2), x3(xt), wdt[0:L, 8 * g:8 * g + 8].unsqueeze(2).to_broadcast([L, 8, 64]), ALU.mult, [r_xt, r_ch], [r_xs2])
            pstB, rpstB = next_trb()
            S.group("pe", [tr(pstB[0:L, 0:128], BT[:, g, cols], identb[:])], [r_BT[g], r_c], [rpstB])
            acopy(Btok[0:L, :], pstB[0:L, 0:128], [rpstB], [r_Btok])
            yield
            psY, rpsY = chain_aux(slot)
            fns = []
            for h in range(8):
                fns.append(mm(psY[0:L, 64 * h:64 * h + 64], MT[0:L, h * L:(h + 1) * L], xd[0:L, 64 * h:64 * h + 64], True, not has_state))
                if has_state:
                    hh = 8 * g + h
                    fns.append(mm(psY[0:L, 64 * h:64 * h + 64], Cp[:, h * L:(h + 1) * L], hTb[:, hh * 64:hh * 64 + 64], False, True))
            S.group("pe", fns, [r_MT, r_xd] + ([r_Cp, r_hTb[g]] if has_state else []), [rpsY])
            psH, rpsH = chain_aux(slot)
            S.group("pe", [mm(psH[:, :], Btok[0:L, :], xs2[0:L, :])], [r_Btok, r_xs2], [rpsH])
            yield
            acopy(ytb[0:L, :], psY[0:L, :], [rpsY], [r_ytb])
            hg = hT[:, g * 512:(g + 1) * 512]
            if has_state:
                vtt(hg.rearrange("p (h d) -> p h d", h=8), hg.rearrange("p (h d) -> p h d", h=8),
                    etot[:, 8 * g:8 * g + 8].unsqueeze(2).to_broadcast([128, 8, 64]), ALU.mult, [r_hT[g], r_ch], [r_hT[g]])
                vtt(hg, hg, psH[:, :], ALU.add, [r_hT[g], rpsH], [r_hT[g]])
            else:
                vcopy(hg, psH[:, :], [rpsH], [r_hT[g]])
            yield
            acopy(hTb[:, g * 512:(g + 1) * 512], hg, [r_hT[g]], [r_hTb[g]])
            pst2, rpst2 = next_trb()
            S.group("pe", [tr(pst2[:, cc * L:(cc + 1) * L], ytb[0:L, cc * 128:(cc + 1) * 128], identb[0:L, 0:L]) for cc in range(4)],
                    [r_ytb, r_c], [rpst2])
            for cc in range(4):
                c = 4 * g + cc
                vop(lambda c=c, cc=cc: V.scalar_tensor_tensor(out=br[:, c, cols], in0=br[:, c, cols], scalar=dcol[:, c:c + 1],
                                                              in1=pst2[:, cc * L:(cc + 1) * L], op0=ALU.mult, op1=ALU.add),
                    [r_br[c], r_par, rpst2], [r_br[c]])
            yield

        def run_chains(makers, width=2):
            active = {}
            nxt = 0
            while nxt < len(makers) or active:
                for s in range(width):
                    if s not in active and nxt < len(makers):
                        active[s] = makers[nxt](s)
                        nxt += 1
                        try:
                            next(active[s])
                        except StopIteration:
                            del active[s]
                for s in list(active.keys()):
                    try:
                        next(active[s])
                    except StopIteration:
                        del active[s]

        def ssd_chunk(q, c0, L, has_state):
            run_chains([(lambda slot, g=g: ssd_gchain(q, c0, L, has_state, g, slot)) for g in range(4)], width=2)

        def ssd_state_out(dst):
            for qd in range(4):
                ps, rps = next_aux()
                S.group("pe", [tr(ps[:, j * 128:(j + 1) * 128], hT[:, (4 * qd + j) * 128:(4 * qd + j + 1) * 128], ident[:]) for j in range(4)],
                        r_hT + [r_c], [rps])
                vcopy(stg[:, qd * 512:(qd + 1) * 512], ps[:, :], [rps], [r_stg])
            S.dma("qsp", dst.rearrange("(c p) n -> p c n", p=128), stg[:].rearrange("p (c n) -> p c n", c=16), reads=[r_stg])

        def ssd_core():
            if seg > 0:
                S.dma("qsp", hT[:], hscr[l], reads=[r_hscr[l]], writes=r_hT)
                acopy(hTb[:], hT[:], r_hT, r_hTb)
            run_chains([(lambda slot, q=q, c0=c0, L=L, g=g: ssd_gchain(q, c0, L, not (seg == 0 and q == 0), g, slot))
                        for q, (c0, L) in enumerate(pchunks) for g in range(4)], width=2)
            if last_seg:
                ssd_state_out(ssm_p[l])
                for r in range(3):
                    S.dma("qsp", conv_p[l, r].rearrange("(c p) -> p c", p=128), convc[:, l, :, r], reads=[r_convc[l]],
                          allow_slow_non_contiguous=True)
            else:
                S.dma("qsp", hscr[l], hT[:], reads=r_hT, writes=[r_hscr[l]])
            for i, (c0, L) in enumerate(schunks):
                S.dma("qsp", stg[:].rearrange("p (c n) -> p c n", c=16), sssm_d[l, i].rearrange("h p n -> (h p) n").rearrange("(c p) n -> p c n", p=128),
                      writes=[r_stg])
                for qd in range(4):
                    ps, rps = next_aux()
                    S.group("pe", [tr(ps[:, j * 128:(j + 1) * 128], stg[:, (4 * qd + j) * 128:(4 * qd + j + 1) * 128], ident[:]) for j in range(4)],
                            [r_stg, r_c], [rps])
                    vcopy(hT[:, qd * 512:(qd + 1) * 512], ps[:, :], [rps], r_hT)
                acopy(hTb[:], hT[:], r_hT, r_hTb)
                ssd_chunk(8 + i, c0, L, True)
                ssd_state_out(ssm_s[l, i])
            if has_s:
                for b in range(4):
                    for r in range(3):
                        S.dma("qsp", conv_s[l, b, r].rearrange("(c p) -> p c", p=128), hso[:, :, b, r], reads=[r_hso],
                              allow_slow_non_contiguous=True)
        J(ssd_core)
        J(S.barrier)
        marks["ssd"] = len(jobs)

        for j in range(4):
            def zcons(cc, outs, j=j):
                c = 4 * j + cc
                for (t0, tn, ps, rps) in outs:
                    i = rot["pjs"] % 2
                    act(szt[i][:, 0:tn], ps[:, 0:tn], AF.Silu, [rps], [r_szt[i]])
                    vtt(br[:, c, t0:t0 + tn], br[:, c, t0:t0 + tn], szt[i][:, 0:tn], ALU.mult, [r_br[c], r_szt[i]], [r_br[c]])
                    rot["pjs"] += 0
            proj_job(jobs, w_in_v, Z0 + 512 * j, 512, hnT, r_hn, tiles, zcons)

        def ssd_norm():
            for g in range(4):
                sumsq_rstd([br[:, 4 * g + cc, 0:ncol] for cc in range(4)], mean512, rstd1, tsq1, rts1, r_br[4 * g:4 * g + 4])
                for cc in range(4):
                    c = 4 * g + cc
                    vop(lambda c=c: V.scalar_tensor_tensor(out=br[:, c, 0:ncol], in0=br[:, c, 0:ncol], scalar=nw[:, c:c + 1],
                                                           in1=rstd1[:, 0:ncol], op0=ALU.mult, op1=ALU.mult),
                        [r_br[c], r_par, r_rstd], [r_br[c]])
        J(ssd_norm)
        marks["ssdnorm"] = len(jobs)
        branch_jobs(cv1b, w_br_ssm_d, GA0, True)
        J(S.barrier)
        marks["p1"] = len(jobs)

        cv2 = Carver()
        rope = cv2.F(4 * 576).rearrange("p (t c) -> p t c", t=4); r_rope = R()
        Sf = cv2.F(2048); r_Sf = [R() for _ in range(8)]
        t1 = [cv2.F(576), cv2.F(576)]; r_t1 = [R(), R()]
        t2 = [cv2.F(576), cv2.F(576)]; r_t2 = [R(), R()]
        qT = cv2.B(8 * 576).rearrange("p (h t) -> p h t", h=8); r_qT = [R() for _ in range(8)]
        kT = cv2.B(8 * 576).rearrange("p (h t) -> p h t", h=8); r_kT = [R() for _ in range(8)]
        ub = [cv2.B(576), cv2.B(576)]; r_ub = [R(), R()]
        Sb = cv2.B(2048); r_Sb = [R() for _ in range(8)]
        NRS = 4
        PT_s = [cv2.B(64, 64) for _ in range(NRS)]; r_PT_s = [R() for _ in range(NRS)]
        vt_s = [cv2.B(256, 64) for _ in range(NRS)]; r_vt_s = [R() for _ in range(NRS)]
        kt_s = [cv2.B(128, 64) for _ in range(NRS)]; r_kt_s = [R() for _ in range(NRS)]
        qp_s = [cv2.B(64) for _ in range(NRS)]; r_qp_s = [R() for _ in range(NRS)]
        rsel = dict(i=0)
        cv2b = Carver()
        rstd2 = cv2b.F(576)
        tsq2 = [cv2b.B(576), cv2b.B(576)]; rts2 = [R(), R()]
        sgt = [cv2b.B(576), cv2b.B(576)]; r_sgt = [R(), R()]

        def ret_pre():
            S.dma("qsp", rope, rope_d[seg].rearrange("t p c -> p t c"), writes=[r_rope])
        J(ret_pre)

        def rot_cons(dstT, r_dst, tc, ts, j):
            def cons(cc, outs):
                h = 4 * j + cc
                for (t0, tn, ps, rps) in outs:
                    i = rot["trb"] % 2
                    rot["trb"] += 1
                    acopy(ub[i][:, 0:tn], ps[:, 0:tn], [rps], [r_ub[i]])
                    ps2, rps2 = ps_for(tn)
                    S.group("pe", [mm(ps2[:, 0:tn], pswapb[:], ub[i][:, 0:tn])], [r_ub[i], r_c], [rps2])
                    vtt(t1[i][:, 0:tn], ps[:, 0:tn], rope[:, tc, t0:t0 + tn], ALU.mult, [rps, r_rope], [r_t1[i]])
                    vtt(t2[i][:, 0:tn], ps2[:, 0:tn], rope[:, ts, t0:t0 + tn], ALU.mult, [rps2, r_rope], [r_t2[i]])
                    vtt(dstT[:, h, t0:t0 + tn], t1[i][:, 0:tn], t2[i][:, 0:tn], ALU.add, [r_t1[i], r_t2[i]], [r_dst[h]])
            return cons
        for j in range(2):
            proj_job(jobs, w_in_v, RQ0 + 512 * j, 512, hnT, r_hn, tiles, rot_cons(qT, r_qT, 0, 1, j))
        for j in range(2):
            proj_job(jobs, w_in_v, RK0 + 512 * j, 512, hnT, r_hn, tiles, rot_cons(kT, r_kT, 2, 3, j))
        for j in range(4):
            def vcons(cc, outs, j=j):
                c = 4 * j + cc
                for (t0, tn, ps, rps) in outs:
                    acopy(br[:, c, t0:t0 + tn], ps[:, 0:tn], [rps], [r_br[c]])
            proj_job(jobs, w_in_v, RV0 + 512 * j, 512, hnT, r_hn, tiles, vcons)

        def ret_gchain(h, c0, L, has_state, slot):
            cols = slice(c0, c0 + L)
            PT, vt, kt, qp = PT_s[slot], vt_s[slot], kt_s[slot], qp_s[slot]
            r_PT, r_vt, r_kt, r_qp = r_PT_s[slot], r_vt_s[slot], r_kt_s[slot], r_qp_s[slot]
            pss, rpss = chain_aux(slot)
            S.group("pe", [mm(pss[0:L, 0:L], kT[:, h, cols], qT[:, h, cols])], [r_kT[h], r_qT[h]], [rpss])
            if has_state:
                vtt(qp[:, 0:L], qT[:, h, cols], g1[:, h * 64:h * 64 + L], ALU.mult, [r_qT[h], r_c], [r_qp])
            yield
            vtt(PT[0:L, 0:L], pss[0:L, 0:L], dm[0:L, h * 64:h * 64 + L], ALU.mult, [rpss, r_c], [r_PT])
            pst, rpst = next_trb()
            S.group("pe", [tr(pst[0:L, 0:128], br[:, 2 * h, cols], identb[:]), tr(pst[0:L, 128:256], br[:, 2 * h + 1, cols], identb[:]),
                           tr(pst[0:L, 256:384], kT[:, h, cols], identb[:])], [r_br[2 * h], r_br[2 * h + 1], r_kT[h], r_c], [rpst])
            acopy(vt[0:L, :], pst[0:L, 0:256], [rpst], [r_vt])
            tcol = h if L == 64 else 8 + h
            vop(lambda: V.tensor_scalar(out=kt[0:L, :], in0=pst[0:L, 256:384], scalar1=te[0:L, tcol:tcol + 1], scalar2=None, op0=ALU.mult),
                [rpst, r_c], [r_kt])
            yield
            pso, rpso = chain_aux(slot)
            fns = []
            for vv in range(2):
                fns.append(mm(pso[:, vv * L:(vv + 1) * L], vt[0:L, vv * 128:(vv + 1) * 128], PT[0:L, 0:L], True, not has_state))
                if has_state:
                    fns.append(mm(pso[:, vv * L:(vv + 1) * L], Sb[:, h * 256 + vv * 128:h * 256 + vv * 128 + 128], qp[:, 0:L], False, True))
            S.group("pe", fns, [r_vt, r_PT] + ([r_Sb[h], r_qp] if has_state else []), [rpso])
            psS, rpsS = chain_aux(slot)
            S.group("pe", [mm(psS[:, 0:256], kt[0:L, :], vt[0:L, :])], [r_kt, r_vt], [rpsS])
            yield
            for vv in range(2):
                acopy(br[:, 2 * h + vv, cols], pso[:, vv * L:(vv + 1) * L], [rpso], [r_br[2 * h + vv]])
            sh = Sf[:, h * 256:(h + 1) * 256]
            if has_state:
                gl = math.exp(LAM[h] * L)
                vop(lambda: V.scalar_tensor_tensor(out=sh, in0=sh, scalar=gl, in1=psS[:, 0:256], op0=ALU.mult, op1=ALU.add),
                    [r_Sf[h], rpsS], [r_Sf[h]])
            else:
                vcopy(sh, psS[:, 0:256], [rpsS], [r_Sf[h]])
            yield
            acopy(Sb[:, h * 256:(h + 1) * 256], sh, [r_Sf[h]], [r_Sb[h]])
            yield

        def ret_chunk(h, c0, L, has_state):
            for _ in ret_gchain(h, c0, L, has_state, 0):
                pass

        def ret_core():
            if seg > 0:
                S.dma("qsp", Sf[:], sscr[l], reads=[r_sscr[l]], writes=r_Sf)
                acopy(Sb[:], Sf[:], r_Sf, r_Sb)
            run_chains([(lambda slot, q=q, c0=c0, L=L, h=h: ret_gchain(h, c0, L, not (seg == 0 and q == 0), slot))
                        for q, (c0, L) in enumerate(pchunks) for h in range(8)], width=3)
            if last_seg:
                S.dma("qsp", ret_p[l].rearrange("h d v -> d h v"), Sf[:].rearrange("p (h v) -> p h v", h=8), reads=r_Sf)
            else:
                S.dma("qsp", sscr[l], Sf[:], reads=r_Sf, writes=[r_sscr[l]])
            for i, (c0, L) in enumerate(schunks):
                S.dma("qsp", Sf[:].rearrange("p (h v) -> p h v", h=8), sret_d[l, i].rearrange("h d v -> d h v"), writes=r_Sf)
                acopy(Sb[:], Sf[:], r_Sf, r_Sb)
                run_chains([(lambda slot, h=h, c0=c0, L=L: ret_gchain(h, c0, L, True, slot)) for h in range(8)], width=2)
                S.dma("qsp", ret_s[l, i].rearrange("h d v -> d h v"), Sf[:].rearrange("p (h v) -> p h v", h=8), reads=r_Sf)
        marks["retproj"] = len(jobs)
        J(ret_core)
        J(S.barrier)
        marks["retcore"] = len(jobs)

        def ret_post():
            for h in range(8):
                sumsq_rstd([br[:, 2 * h + vv, 0:ncol] for vv in range(2)], mean256, rstd2, tsq2, rts2, r_br[2 * h:2 * h + 2])
                for vv in range(2):
                    c = 2 * h + vv
                    vtt(br[:, c, 0:ncol], br[:, c, 0:ncol], rstd2[:, 0:ncol], ALU.mult, [r_br[c], r_rstd], [r_br[c]])
        J(ret_post)
        for j in range(4):
            def gcons2(cc, outs, j=j):
                c = 4 * j + cc
                for (t0, tn, ps, rps) in outs:
                    i = rot["trb"] % 2
                    rot["trb"] += 1
                    act(sgt[i][:, 0:tn], ps[:, 0:tn], AF.Silu, [rps], [r_sgt[i]])
                    vtt(br[:, c, t0:t0 + tn], br[:, c, t0:t0 + tn], sgt[i][:, 0:tn], ALU.mult, [r_br[c], r_sgt[i]], [r_br[c]])
            proj_job(jobs, w_in_v, RG0 + 512 * j, 512, hnT, r_hn, tiles, gcons2)
        branch_jobs(cv2b, w_br_ret_d, GB0, False)
        J(S.barrier)
        marks["p2"] = len(jobs)

        cv3 = Carver()
        _bt0 = cv3.F(576, 64); _rbt0 = R()
        bt = [_bt0, _bt0]; r_bt = [_rbt0, _rbt0]
        if has_s:
            _btr0 = cv3.F(576, 64); _rbtr0 = R()
            btr = [_btr0, _btr0]; r_btr = [_rbtr0, _rbtr0]
        else:
            btr = [cv3.F(576, 64), cv3.F(576, 64)]; r_btr = [R(), R()]
        NAB = 4
        tS2 = [cv3.F(512, 64) for _ in range(NAB)]; r_tS2 = [R() for _ in range(NAB)]
        tS = tS2[0]; r_tS = r_tS2[0]
        tB = cv3.F(64, 64); r_tB = R()
        rd = cv3.F(512); r_rd = R()
        _kf0 = cv3.F(576); _rkf0 = R()
        kf2 = [_kf0, _kf0]; r_kf2 = [_rkf0, _rkf0]
        vf = cv3.F(576); r_vf = R()
        ostg = cv3.F(512); r_ostg = R()
        rs32 = [cv3.F(576), cv3.F(576)]; r_rs32 = [R(), R()]
        nsel = dict(i=0)
        qaT = cv3.B(4 * 576).rearrange("p (h t) -> p h t", h=4); r_qa = [R() for _ in range(4)]
        kaT = cv3.B(4 * 576).rearrange("p (h t) -> p h t", h=4); r_ka = [R() for _ in range(4)]
        if seg > 0:
            khist2 = [cv3.B(512), cv3.B(512)]
            vhistT2 = [cv3.B(512), cv3.B(512)]
        else:
            khist2 = [None, None]
            vhistT2 = [None, None]
        r_khist2 = [R(), R()]; r_vhist2 = [R(), R()]
        _vt0 = cv3.B(16 * 128, 64).rearrange("p (k d) -> p k d", k=16); _rvt0 = R()
        vtok2 = [_vt0, _vt0]; r_vtok2 = [_rvt0, _rvt0]
        ktokh = cv3.B(8 * 128, 64).rearrange("p (k d) -> p k d", k=8); r_ktokh = R()
        vtokh = cv3.B(8 * 128, 64).rearrange("p (k d) -> p k d", k=8); r_vtokh = R()
        khT = cv3.B(512); r_khT = R()
        vnew = cv3.B(128, 64); r_vnew = R()
        PT2 = [cv3.B(512, 64) for _ in range(NAB)]; r_PT2 = [R() for _ in range(NAB)]
        PTa = PT2[0]; r_PTa = r_PT2[0]
        PTb = cv3.B(64, 64); r_PTb = R()
        samp = [dict(ktokh=ktokh, vtokh=vtokh, khT=khT, vnew=vnew, tS=tS2[0], tB=tB, PTa=PT2[0], PTb=PTb, rd=rd,
                     r=dict(ktokh=r_ktokh, vtokh=r_vtokh, khT=r_khT, vnew=r_vnew, tS=r_tS2[0], tB=r_tB, PTa=r_PT2[0], PTb=r_PTb, rd=r_rd),
                     banks=[(aux[0], r_aux[0]), (aux[1], r_aux[1]), (pj[0], r_pj[0])])]
        if has_s:
            samp.append(dict(ktokh=cv3.B(8 * 128, 64).rearrange("p (k d) -> p k d", k=8),
                             vtokh=cv3.B(8 * 128, 64).rearrange("p (k d) -> p k d", k=8),
                             khT=cv3.B(512), vnew=cv3.B(128, 64), tS=tS2[1], tB=cv3.F(64, 64), PTa=PT2[1], PTb=cv3.B(64, 64),
                             rd=cv3.F(16),
                             r=dict(ktokh=R(), vtokh=R(), khT=R(), vnew=R(), tS=r_tS2[1], tB=R(), PTa=r_PT2[1], PTb=R(), rd=R()),
                             banks=[(aux[2], r_aux[2]), (aux[3], r_aux[3]), (pj[1], r_pj[1])]))
        sq3 = [cv3.B(576), cv3.B(576)]; r_sq3 = [R(), R()]
        par = seg % 2

        def norm_cons(dstT, r_dst, wcol, keep_f32, j):
            def cons(cc, outs):
                kf, r_kf = kf2[nsel["i"] % 2], r_kf2[nsel["i"] % 2]
                nsel["i"] += 1
                for (t0, tn, ps, rps) in outs:
                    i = rot["trb"] % 2
                    rot["trb"] += 1
                    rs3, r_rs3 = rs32[i], r_rs32[i]
                    act(sq3[i][:, 0:tn], ps[:, 0:tn], AF.Square, [rps], [r_sq3[i]])
                    ps2, rps2 = ps_for(tn)
                    S.group("pe", [mm(ps2[:, 0:tn], mean128[:], sq3[i][:, 0:tn])], [r_sq3[i], r_c], [rps2])
                    act(rs3[:, 0:tn], ps2[:, 0:tn], AF.Ln, [rps2], [r_rs3], bias=EPS, scale=1.0)
                    act(rs3[:, 0:tn], rs3[:, 0:tn], AF.Exp, [r_rs3], [r_rs3], scale=-0.5)
                    if keep_f32:
                        vop(lambda: V.scalar_tensor_tensor(out=kf[:, t0:t0 + tn], in0=ps[:, 0:tn], scalar=wcol[:, 0:1], in1=rs3[:, 0:tn],
                                                           op0=ALU.mult, op1=ALU.mult), [rps, r_par, r_rs3], [r_kf])
                        acopy(dstT[:, cc, t0:t0 + tn], kf[:, t0:t0 + tn], [r_kf], [r_dst[cc]])
                    else:
                        vop(lambda: V.scalar_tensor_tensor(out=dstT[:, cc, t0:t0 + tn], in0=ps[:, 0:tn], scalar=wcol[:, 0:1], in1=rs3[:, 0:tn],
                                                           op0=ALU.mult, op1=ALU.mult), [rps, r_par, r_rs3], [r_dst[cc]])
                if keep_f32:
                    kv_out(kf, r_kf, k_p, k_s, 4 * j + cc)
            return cons

        def kv_out(src, r_src, dst_p, dst_s, h):
            if last_seg:
                ps, rps = next_aux()
                S.group("pe", [tr(ps[:, tt * 128:(tt + 1) * 128], src[:, tt * 128:(tt + 1) * 128], ident[:]) for tt in range(4)],
                        [r_src, r_c], [rps])
                vcopy(ostg[:, :], ps[:, :], [rps], [r_ostg])
                S.dma("qsp", dst_p[l].rearrange("(tt p) h d -> p tt h d", p=128)[:, :, h, :], ostg[:, :].rearrange("p (tt d) -> p tt d", tt=4),
                      reads=[r_ostg])
            if has_s:
                ps, rps = next_aux()
                S.group("pe", [tr(ps[0:64, 0:128], src[:, 512:576], ident[:])], [r_src, r_c], [rps])
                vcopy(ostg[0:64, 0:128], ps[0:64, 0:128], [rps], [r_ostg])
                S.dma("qsp", dst_s[l][:, h, :], ostg[0:64, 0:128], reads=[r_ostg])

        def attend(qap, Lq, keys, bb0, outap, r_q, r_keys, r_out, ib):
            nblk = len(keys)
            n8 = min(nblk, 8)
            psA, rpsA = next_aux()
            psB, rpsB = next_aux()
            fns = []
            for n, (kap, vap, nk) in enumerate(keys):
                tgt = psA[0:nk, n * Lq:(n + 1) * Lq] if n < 8 else psB[0:nk, 0:Lq]
                fns.append(mm(tgt, kap, qap))
            S.group("pe", fns, [r_q] + r_keys, [rpsA, rpsB])
            btv = bt[ib].rearrange("p (b i) -> p b i", b=9)
            vop(lambda: V.scalar_tensor_tensor(out=tS[:, 0:n8 * Lq].rearrange("p (n i) -> p n i", n=n8),
                                               in0=psA[0:64, 0:n8 * Lq].rearrange("p (n i) -> p n i", n=n8), scalar=ATT_SCALE,
                                               in1=btv[:, bb0:bb0 + n8, 0:Lq], op0=ALU.mult, op1=ALU.add),
                [rpsA, r_bt[ib]], [r_tS])
            act(PTa[:, 0:n8 * Lq], tS[:, 0:n8 * Lq], AF.Exp, [r_tS], [r_PTa])
            if nblk == 9:
                nk = keys[8][2]
                vop(lambda: V.scalar_tensor_tensor(out=tB[0:nk, 0:Lq], in0=psB[0:nk, 0:Lq], scalar=ATT_SCALE,
                                                   in1=btv[0:nk, bb0 + 8, 0:Lq], op0=ALU.mult, op1=ALU.add),
                    [rpsB, r_bt[ib]], [r_tB])
                act(PTb[0:nk, 0:Lq], tB[0:nk, 0:Lq], AF.Exp, [r_tB], [r_PTb])
            psO, rpsO = next_aux()
            fns = []
            for n, (kap, vap, nk) in enumerate(keys):
                pt = PTa[0:nk, n * Lq:(n + 1) * Lq] if n < 8 else PTb[0:nk, 0:Lq]
                fns.append(mm(psO[:, 0:Lq], vap, pt, n == 0, n == nblk - 1))
            for n, (kap, vap, nk) in enumerate(keys):
                pt = PTa[0:nk, n * Lq:(n + 1) * Lq] if n < 8 else PTb[0:nk, 0:Lq]
                fns.append(mm(psO[:, 64:64 + Lq], ones64b[0:nk, :], pt, n == 0, n == nblk - 1))
            S.group("pe", fns, [r_PTa, r_PTb, r_c] + r_keys, [rpsO])
            act(rd[:, 0:Lq], psO[:, 64:64 + Lq], AF.Ln, [rpsO], [r_rd])
            act(rd[:, 0:Lq], rd[:, 0:Lq], AF.Exp, [r_rd], [r_rd], scale=-1.0)
            vtt(outap, psO[:, 0:Lq], rd[:, 0:Lq], ALU.mult, [rpsO, r_rd], [r_out])

        def att_heads(j):
            for cc in range(4):
                h = 4 * j + cc
                ib = h % 2
                S.dma("qsp", btr[ib].rearrange("p (b i) -> p b i", b=9), biasr_d[l, h].rearrange("(b jj) i -> jj b i", jj=64),
                      writes=[r_btr[ib]])
                if has_s:
                    S.dma("qsp", bt[ib].rearrange("p (b i) -> p b i", b=9), bias_d[l, h].rearrange("(b jj) i -> jj b i", jj=64),
                          writes=[r_bt[ib]])
                kcs0 = 0 if seg > 0 else 8
                khist, vhistT, vtok = khist2[ib], vhistT2[ib], vtok2[ib]
                r_khist, r_vhist, r_vtok = r_khist2[ib], r_vhist2[ib], r_vtok2[ib]
                if seg > 0:
                    S.dma("qsp", khist[:, :], khd[l, 1 - par][:, h, :], reads=[r_khd[l][1 - par][j]], writes=[r_khist])
                    S.dma("qsp", vhistT[:, :], vhd[l, 1 - par][:, h, :], reads=[r_vhd[l][1 - par][j]], writes=[r_vhist])
                for k4 in range(kcs0 // 4, 4):
                    pst, rpst = next_trb()
                    fns = []
                    for jj in range(4):
                        kc = 4 * k4 + jj
                        src = vhistT[:, kc * 64:(kc + 1) * 64] if kc < 8 else br[:, h, (kc - 8) * 64:(kc - 7) * 64]
                        fns.append(tr(pst[0:64, jj * 128:(jj + 1) * 128], src, identb[:]))
                    S.group("pe", fns, [r_vhist, r_br[h], r_c], [rpst])
                    acopy(vtok[:, 4 * k4:4 * k4 + 4, :], pst[0:64, :].rearrange("p (k d) -> p k d", k=4), [rpst], [r_vtok])
                kcs = list(range(kcs0, 16))
                psO, rO = aux[2], r_aux[2]
                psD, rD = aux[3], r_aux[3]
                btrv = btr[ib].rearrange("p (b i) -> p b i", b=9)

                def rng(kc):
                    c_lo = max(0, kc - 8)
                    c_hi = min(7, kc)
                    return c_lo, c_hi, c_hi - c_lo + 1

                sbank = [(aux[0], r_aux[0]), (aux[1], r_aux[1]), (pj[0], r_pj[0]), (pj[1], r_pj[1])]
                LOOK = 3

                def emitS(idx, cc=cc, khist=khist, r_khist=r_khist):
                    kc = kcs[idx]
                    c_lo, c_hi, n = rng(kc)
                    sb_, rsb_ = sbank[idx % NAB]
                    kap = khist[:, kc * 64:(kc + 1) * 64] if kc < 8 else kaT[:, cc, (kc - 8) * 64:(kc - 7) * 64]
                    S.group("pe", [mm(sb_[0:64, 0:n * 64], kap, qaT[:, cc, c_lo * 64:(c_hi + 1) * 64])],
                            [r_khist, r_ka[cc], r_qa[cc]], [rsb_])
                for idx in range(min(LOOK, len(kcs))):
                    emitS(idx)
                for idx, kc in enumerate(kcs):
                    if idx + LOOK < len(kcs):
                        emitS(idx + LOOK)
                    c_lo, c_hi, n = rng(kc)
                    p = idx % NAB
                    sb_, rsb_ = sbank[p]
                    r_lo = 8 - kc + c_lo
                    vop(lambda p=p, n=n, r_lo=r_lo, sb_=sb_: V.scalar_tensor_tensor(
                        out=tS2[p][:, 0:n * 64].rearrange("p (n i) -> p n i", n=n),
                        in0=sb_[0:64, 0:n * 64].rearrange("p (n i) -> p n i", n=n), scalar=ATT_SCALE,
                        in1=btrv[:, r_lo:r_lo + n, :], op0=ALU.mult, op1=ALU.add), [rsb_, r_btr[ib]], [r_tS2[p]])
                    act(PT2[p][:, 0:n * 64], tS2[p][:, 0:n * 64], AF.Exp, [r_tS2[p]], [r_PT2[p]])
                    first = idx == 0
                    last = idx == len(kcs) - 1
                    S.group("pe", [mm(psO[:, c_lo * 64:(c_hi + 1) * 64], vtok[:, kc, :], PT2[p][:, 0:n * 64], first, last, skip=True),
                                   mm(psD[:, c_lo * 64:(c_hi + 1) * 64], ones64b[:, :], PT2[p][:, 0:n * 64], first, last, skip=True)],
                            [r_PT2[p], r_vtok, r_c], [rO, rD])
                act(rd[:, 0:512], psD[:, 0:512], AF.Ln, [rD], [r_rd])
                act(rd[:, 0:512], rd[:, 0:512], AF.Exp, [r_rd], [r_rd], scale=-1.0)
                vtt(br[:, h, 0:512], psO[:, 0:512], rd[:, 0:512], ALU.mult, [rO, r_rd], [r_br[h]])
                def samp_chain(i, c0, L, slot, cc=cc, h=h, ib=ib):
                    sbf = samp[slot]
                    rr = sbf["r"]
                    ktokh_, vtokh_, khT_, vnew_ = sbf["ktokh"], sbf["vtokh"], sbf["khT"], sbf["vnew"]
                    S.dma("qpool", ktokh_, ck_d[l, i][:, h, :].rearrange("(b jj) d -> jj b d", jj=64), writes=[rr["ktokh"]])
                    S.dma("qpool", vtokh_, cv_d[l, i][:, h, :].rearrange("(b jj) d -> jj b d", jj=64), writes=[rr["vtokh"]])
                    yield
                    pst, rpst = next_trb()
                    S.group("pe", [tr(pst[:, b_ * 64:(b_ + 1) * 64], ktokh_[:, b_, :], identb[0:64, 0:64]) for b_ in range(8)],
                            [rr["ktokh"], r_c], [rpst])
                    acopy(khT_[:, :], pst[:, :], [rpst], [rr["khT"]])
                    yield
                    pst2, rpst2 = next_trb()
                    S.group("pe", [tr(pst2[0:L, 0:128], br[:, h, c0:c0 + L], identb[:])], [r_br[h], r_c], [rpst2])
                    acopy(vnew_[0:L, :], pst2[0:L, 0:128], [rpst2], [rr["vnew"]])
                    yield
                    keys = [(khT_[:, b_ * 64:(b_ + 1) * 64], vtokh_[:, b_, :], 64) for b_ in range(8)]
                    keys.append((kaT[:, cc, c0:c0 + L], vnew_[0:L, :], L))
                    qap = qaT[:, cc, c0:c0 + L]
                    Lq = L
                    (psA, rpsA), (psB, rpsB), (psO, rpsO) = sbf["banks"]
                    tS_, tB_, PTa_, PTb_, rd_ = sbf["tS"], sbf["tB"], sbf["PTa"], sbf["PTb"], sbf["rd"]
                    r_keys = [rr["khT"], rr["vtokh"], r_ka[cc], rr["vnew"]]
                    fns = []
                    for n, (kap, vap, nk) in enumerate(keys):
                        tgt = psA[0:nk, n * Lq:(n + 1) * Lq] if n < 8 else psB[0:nk, 0:Lq]
                        fns.append(mm(tgt, kap, qap))
                    S.group("pe", fns, [r_qa[cc]] + r_keys, [rpsA, rpsB])
                    yield
                    btv = bt[ib].rearrange("p (b i) -> p b i", b=9)
                    vop(lambda: V.scalar_tensor_tensor(out=tS_[:, 0:8 * Lq].rearrange("p (n i) -> p n i", n=8),
                                                       in0=psA[0:64, 0:8 * Lq].rearrange("p (n i) -> p n i", n=8), scalar=ATT_SCALE,
                                                       in1=btv[:, 0:8, 0:Lq], op0=ALU.mult, op1=ALU.add),
                        [rpsA, r_bt[ib]], [rr["tS"]])
                    nk9 = keys[8][2]
                    vop(lambda: V.scalar_tensor_tensor(out=tB_[0:nk9, 0:Lq], in0=psB[0:nk9, 0:Lq], scalar=ATT_SCALE,
                                                       in1=btv[0:nk9, 8, 0:Lq], op0=ALU.mult, op1=ALU.add),
                        [rpsB, r_bt[ib]], [rr["tB"]])
                    yield
                    act(PTa_[:, 0:8 * Lq], tS_[:, 0:8 * Lq], AF.Exp, [rr["tS"]], [rr["PTa"]])
                    act(PTb_[0:nk9, 0:Lq], tB_[0:nk9, 0:Lq], AF.Exp, [rr["tB"]], [rr["PTb"]])
                    yield
                    fns = []
                    for n, (kap, vap, nk) in enumerate(keys):
                        pt = PTa_[0:nk, n * Lq:(n + 1) * Lq] if n < 8 else PTb_[0:nk, 0:Lq]
                        fns.append(mm(psO[:, 0:Lq], vap, pt, n == 0, n == 8))
                    for n, (kap, vap, nk) in enumerate(keys):
                        pt = PTa_[0:nk, n * Lq:(n + 1) * Lq] if n < 8 else PTb_[0:nk, 0:Lq]
                        fns.append(mm(psO[:, 64:64 + Lq], ones64b[0:nk, :], pt, n == 0, n == 8))
                    S.group("pe", fns, [rr["PTa"], rr["PTb"], r_c] + r_keys, [rpsO])
                    yield
                    act(rd_[:, 0:Lq], psO[:, 64:64 + Lq], AF.Ln, [rpsO], [rr["rd"]])
                    yield
                    act(rd_[:, 0:Lq], rd_[:, 0:Lq], AF.Exp, [rr["rd"]], [rr["rd"]], scale=-1.0)
                    yield
                    vtt(br[:, h, c0:c0 + L], psO[:, 0:Lq], rd_[:, 0:Lq], ALU.mult, [rpsO, rr["rd"]], [r_br[h]])
                    yield
                if schunks:
                    run_chains([(lambda slot, i=i, c0=c0, L=L: samp_chain(i, c0, L, slot)) for i, (c0, L) in enumerate(schunks)], width=2)

        for j in range(4):
            proj_job(jobs, w_in_v, AQ0 + 512 * j, 512, hnT, r_hn, tiles, norm_cons(qaT, r_qa, qn, False, j))
            proj_job(jobs, w_in_v, AK0 + 512 * j, 512, hnT, r_hn, tiles, norm_cons(kaT, r_ka, kn, True, j))

            def avcons(cc, outs, j=j):
                h = 4 * j + cc
                for (t0, tn, ps, rps) in outs:
                    acopy(vf[:, t0:t0 + tn], ps[:, 0:tn], [rps], [r_vf])
                    vcopy(br[:, h, t0:t0 + tn], ps[:, 0:tn], [rps], [r_br[h]])
                kv_out(vf, r_vf, v_p, v_s, h)
            proj_job(jobs, w_in_v, AV0 + 512 * j, 512, hnT, r_hn, tiles, avcons)

            def hist_save(j=j):
                if not last_seg:
                    S.dma("qsp", khd[l, par][:, 4 * j:4 * j + 4, :], kaT[:, :, 0:512], reads=r_ka, writes=[r_khd[l][par][j]])
                    S.dma("qsp", vhd[l, par][:, 4 * j:4 * j + 4, :], br[:, 4 * j:4 * j + 4, 0:512], reads=r_br[4 * j:4 * j + 4],
                          writes=[r_vhd[l][par][j]])
                att_heads(j)
            J(hist_save)
        J(S.barrier)
        marks["att"] = len(jobs)
        branch_jobs(Carver(), w_br_att_d, GC0, False)
        J(S.barrier)
        marks["p3"] = len(jobs)

        wov = wview(w_out_d[l])
        for j in range(4):
            def ocons(cc, outs, j=j):
                c = 4 * j + cc
                for (t0, tn, ps, rps) in outs:
                    vtt(xT[:, c, t0:t0 + tn], xT[:, c, t0:t0 + tn], ps[:, 0:tn], ALU.add, [r_x[c], rps], [r_x[c]])
            proj_job(jobs, wov, 512 * j, 512, mT, r_m, tiles, ocons)

        cv4 = Carver()
        rstd4 = cv4.F(576)
        tsq4 = [cv4.B(576), cv4.B(576)]; rts4 = [R(), R()]
        sa = [cv4.B(4 * 576).rearrange("p (c t) -> p c t", c=4) for _ in range(2)]; r_sa = [R(), R()]
        actb = [cv4.B(4 * 576).rearrange("p (c t) -> p c t", c=4) for _ in range(2)]; r_actb = [R(), R()]
        yst = cv4.F(2048); r_yst = R()

        def norm2():
            sumsq_rstd([xT[:, c, 0:ncol] for c in range(16)], meanD, rstd4, tsq4, rts4, r_x)
            for c in range(16):
                vop(lambda c=c: V.scalar_tensor_tensor(out=hnT[:, c, 0:ncol], in0=xT[:, c, 0:ncol], scalar=gf[:, c:c + 1],
                                                       in1=rstd4[:, 0:ncol], op0=ALU.mult, op1=ALU.mult),
                    [r_x[c], r_par, r_rstd], [r_hn[c]])
        J(norm2)
        wfv = wview(w_ffn_in_d[l])
        wfo = w_ffn_out_d[l].rearrange("(kc p) c -> p kc c", p=128)
        for j in range(11):
            i = j % 2

            def acons(cc, outs, i=i):
                for (t0, tn, ps, rps) in outs:
                    act(sa[i][:, cc, t0:t0 + tn], ps[:, 0:tn], AF.Silu, [rps], [r_sa[i]])
            proj_job(jobs, wfv, 512 * j, 512, hnT, r_hn, tiles, acons)

            def ccons(cc, outs, i=i):
                for (t0, tn, ps, rps) in outs:
                    vtt(actb[i][:, cc, t0:t0 + tn], ps[:, 0:tn], sa[i][:, cc, t0:t0 + tn], ALU.mult, [rps, r_sa[i]], [r_actb[i]])
            proj_job(jobs, wfv, FFH + 512 * j, 512, hnT, r_hn, tiles, ccons)

            def ld(slot, j=j):
                sv = slabs[slot][:, :].rearrange("p (k c) -> p k c", k=4)
                S.dma("qpool", sv, wfo[:, 4 * j:4 * j + 4, :], writes=[r_slab[slot]])

            def fn(slot, i=i):
                sv = slabs[slot][:, :].rearrange("p (k c) -> p k c", k=4)
                for oc in range(16):
                    for (t0, tn) in tiles:
                        ps, rps = ps_for(tn)
                        S.group("pe", [mm(ps[:, 0:tn], sv[:, k, oc * 128:(oc + 1) * 128], actb[i][:, k, t0:t0 + tn], k == 0, k == 3)
                                       for k in range(4)], [r_slab[slot], r_actb[i]], [rps])
                        vtt(xT[:, oc, t0:t0 + tn], xT[:, oc, t0:t0 + tn], ps[:, 0:tn], ALU.add, [r_x[oc], rps], [r_x[oc]])
            jobs.append((ld, fn))

        def y_out():
            dsts = [(y_p[seg * 512 + 128 * tt: seg * 512 + 128 * tt + 128, :], 128, 128 * tt) for tt in range(4)]
            if has_s:
                dsts.append((y_s[:, :], 64, 512))
            for (dst, nt, c0) in dsts:
                for q in range(4):
                    ps, rps = next_aux()
                    S.group("pe", [tr(ps[0:nt, jx * 128:(jx + 1) * 128], xT[:, 4 * q + jx, c0:c0 + nt], ident[:]) for jx in range(4)],
                            r_x[4 * q:4 * q + 4] + [r_c], [rps])
                    vcopy(yst[0:nt, q * 512:(q + 1) * 512], ps[0:nt, :], [rps], [r_yst])
                S.dma("qsp", dst, yst[0:nt, :], reads=[r_yst])
        if last_layer:
            J(y_out)
        J(S.barrier)
        stop = os.environ.get("KSTOP")
        if stop:
            jobs = jobs[:marks[stop]]
            jobs.append((None, lambda _s: S.barrier()))
        return jobs

    all_jobs = []
    for seg in range(n_segs):
        for l in range(n_layers):
            all_jobs.extend(layer_pass(l, seg, l == n_layers - 1))
    run_jobs(all_jobs)
    S.finish()
    return nc


def _consts(rel_bias):
    f32 = np.float32
    half = 64
    inv = (np.float32(10000.0) ** (-np.arange(half, dtype=f32) * f32(2.0) / f32(128))).astype(f32)
    rope = np.zeros((4, 4, 128, 576), f32)
    for seg in range(4):
        pos = np.concatenate([seg * 512 + np.arange(512), 1024 + (np.arange(64) % 16)]).astype(f32)
        ang = (pos[:, None] * inv[None, :]).astype(f32)
        c = np.cos(ang).astype(f32).T
        s = np.sin(ang).astype(f32).T
        cos_full = np.concatenate([c, c], 0)
        sin_full = np.concatenate([-s, s], 0)
        sc = f32(128 ** -0.5)
        rope[seg, 0] = cos_full
        rope[seg, 1] = sin_full
        rope[seg, 2] = cos_full * sc
        rope[seg, 3] = sin_full * sc
    lam = np.log1p(-np.exp2(-5.0 - np.arange(8, dtype=f32))).astype(f32)
    i = np.arange(64, dtype=f32)
    diff = i[None, :] - i[:, None]
    dm = np.where(diff[:, None, :] >= 0, np.exp(lam[None, :, None] * np.maximum(diff[:, None, :], 0.0)), 0.0).astype(f32)
    g1 = np.exp(lam[:, None] * (i + 1.0)[None, :]).astype(f32)
    g1 = np.broadcast_to(g1.reshape(1, 512), (128, 512)).copy()
    te = np.zeros((64, 16), f32)
    te[:, 0:8] = np.exp(lam[None, :] * (63.0 - i)[:, None])
    te[0:16, 8:16] = np.exp(lam[None, :] * (15.0 - i[0:16])[:, None])
    jj = np.arange(576)[:, None]
    ii = np.arange(64)[None, :]
    idx = np.clip(512 + ii - jj, -256, 256) + 256
    biasT = np.ascontiguousarray(rel_bias[:, :, idx]).astype(f32)
    biasR = np.ascontiguousarray(biasT.reshape(2, 16, 9, 64, 64)[:, :, ::-1].reshape(2, 16, 576, 64))
    tri = (np.arange(64)[:, None] <= np.arange(64)[None, :]).astype(f32)
    pswap = np.zeros((128, 128), f32)
    for m in range(128):
        pswap[(m + 64) % 128, m] = 1.0
    return dict(c_rope=rope, c_biasT=biasT, c_biasR=biasR, c_dm=dm.reshape(64, 512), c_g1=g1, c_te=te,
                c_ident=np.eye(128, dtype=f32), c_tri=tri, c_pswap=pswap)


_NC_CACHE = {}


def kernel(x_prompt, x_sample, cache_attn_k, cache_attn_v, state_ret, state_ssm, state_conv,
           norm_mix, w_in, conv_w, conv_b, dt_bias, a_log, d_skip, ssm_norm, q_norm, k_norm,
           rel_bias, w_br_ssm, w_br_ret, w_br_att, w_out, norm_ffn, w_ffn_in, w_ffn_out,
           _n_layers=2, _n_segs=4, _cores=8):
    f = lambda a: np.ascontiguousarray(np.asarray(a, dtype=np.float32))
    key = (_n_layers, _n_segs)
    if key not in _NC_CACHE:
        _NC_CACHE[key] = build(_n_layers, _n_segs)
    nc = _NC_CACHE[key]
    consts = _consts(f(rel_bias))
    shared = dict(norm_mix=f(norm_mix), w_in=f(w_in), conv_w=f(conv_w), conv_b=f(conv_b), dt_bias=f(dt_bias), a_log=f(a_log),
                  d_skip=f(d_skip), ssm_norm=f(ssm_norm), q_norm=f(q_norm), k_norm=f(k_norm), w_br_ssm=f(w_br_ssm),
                  w_br_ret=f(w_br_ret), w_br_att=f(w_br_att), w_out=f(w_out), norm_ffn=f(norm_ffn), w_ffn_in=f(w_ffn_in),
                  w_ffn_out=f(w_ffn_out), **consts)
    x_prompt = f(x_prompt); x_sample = f(x_sample)
    ck = f(cache_attn_k); cvv = f(cache_attn_v); sr = f(state_ret); ss = f(state_ssm); sc = f(state_conv)
    in_maps = []
    for c in range(_cores):
        b = c % 4
        sl = slice(4 * c, 4 * c + 4)
        m = dict(shared)
        m.update(x_prompt=x_prompt[b], x_sample=x_sample[sl], cache_attn_k=np.ascontiguousarray(ck[:, sl]),
                 cache_attn_v=np.ascontiguousarray(cvv[:, sl]), state_ret=np.ascontiguousarray(sr[:, sl]),
                 state_ssm=np.ascontiguousarray(ss[:, sl]), state_conv=np.ascontiguousarray(sc[:, sl]))
        in_maps.append(m)
    res = run_bass_kernel_spmd(nc, in_maps, core_ids=list(range(_cores)))
    rs = res.results
    npc = min(4, _cores)
    yp = np.stack([rs[b]["y_p"] for b in range(npc)])
    ys = np.concatenate([rs[c]["y_s"].reshape(4, 16, D) for c in range(_cores)])
    pk = np.stack([rs[b]["k_p"] for b in range(npc)], 1)
    pv = np.stack([rs[b]["v_p"] for b in range(npc)], 1)
    pr = np.stack([rs[b]["ret_p"] for b in range(npc)], 1)
    pssm = np.stack([rs[b]["ssm_p"].reshape(2, 32, 64, 128) for b in range(npc)], 1)
    pconv = np.stack([rs[b]["conv_p"] for b in range(npc)], 1)
    sk = np.concatenate([rs[c]["k_s"].reshape(2, 4, 16, 16, 128) for c in range(_cores)], 1)
    sv = np.concatenate([rs[c]["v_s"].reshape(2, 4, 16, 16, 128) for c in range(_cores)], 1)
    srr = np.concatenate([rs[c]["ret_s"] for c in range(_cores)], 1)
    sssm = np.concatenate([rs[c]["ssm_s"].reshape(2, 4, 32, 64, 128) for c in range(_cores)], 1)
    sconv = np.concatenate([rs[c]["conv_s"] for c in range(_cores)], 1)
    return (yp, ys, pk, pv, pr, pssm, pconv, sk, sv, srr, sssm, sconv)
```
